# Optimizing a Trainium2 kernel written in Bass

```python
import jax
import jax.numpy as jnp
from jax import lax
import numpy as np

D_MODEL = 2048
BATCH = 4
SEQ = 2048
DEPTH = 2

GRID_W = 64
CTX_LEN = 256
NORM_EPS = 1e-6
ROPE_BASE = 10000.0
Q_BLOCK = 128

NA_HEADS = 8
NA_HEAD_DIM = 128
NA_WIDTH = NA_HEADS * NA_HEAD_DIM
NA_WIN_ROWS = 8
NA_WIN_COLS = 16

RW_HEAD_DIM = 64
RW_HEADS = 16
RW_WIDTH = RW_HEADS * RW_HEAD_DIM
RW_DECAY_RANK = 64
RW_ICLR_RANK = 64
RW_SHIFT_COLS = 3 * RW_WIDTH + 2 * RW_DECAY_RANK + 2 * RW_ICLR_RANK
RW_GN_EPS = 64e-5

MLA_HEADS = 8
MLA_Q_RANK = 512
MLA_KV_RANK = 512
MLA_NOPE_DIM = 128
MLA_ROPE_DIM = 64
MLA_V_DIM = 128
MLA_WIDTH = MLA_HEADS * MLA_V_DIM

HG_HEADS = 8
HG_EXPAND = 128
HG_HEAD_I = 128
HG_FDIM = HG_HEADS * HG_EXPAND
HG_WIDTH = HG_HEADS * HG_HEAD_I
HG_CHUNK = 32

EVEN_MIX = NA_WIDTH + RW_WIDTH
EVEN_IN = 3 * NA_WIDTH + RW_SHIFT_COLS + EVEN_MIX
ODD_MIX = MLA_WIDTH + HG_WIDTH
ODD_IN = MLA_Q_RANK + MLA_KV_RANK + MLA_ROPE_DIM + 3 * HG_FDIM + HG_WIDTH + ODD_MIX

kernel_name = 'hybrid_na_rwkv7_mla_hgrn2_dit'


def rms_norm(x, w):
    xf = x.astype(jnp.float32)
    y = xf * lax.rsqrt(jnp.mean(jnp.square(xf), axis=-1, keepdims=True) + NORM_EPS)
    return (y * w.astype(jnp.float32)).astype(x.dtype)


def adaln(cond, w, b):
    m = jax.nn.silu(cond) @ w + b
    return jnp.split(m, 3, axis=-1)


def heads(t, n):
    B, T, _ = t.shape
    return t.reshape(B, T, n, -1).transpose(0, 2, 1, 3)


def merge(t):
    B, n, T, d = t.shape
    return t.transpose(0, 2, 1, 3).reshape(B, T, n * d)


def softmax_f32(s, dtype):
    return jax.nn.softmax(s.astype(jnp.float32), axis=-1).astype(dtype)


def dense_attention(q, k, v):
    s = jnp.einsum('bhqd,bhkd->bhqk', q, k) * (q.shape[-1] ** -0.5)
    return jnp.einsum('bhqk,bhkd->bhqd', softmax_f32(s, v.dtype), v)


def axial_rope_angles(n_tokens, dim):
    t = jnp.arange(n_tokens, dtype=jnp.int32)
    pos = jnp.stack([t // GRID_W, t % GRID_W], axis=-1).astype(jnp.float32)
    n_freq = dim // 4
    inv = ROPE_BASE ** (-jnp.arange(n_freq, dtype=jnp.float32) / n_freq)
    return (pos[:, :, None] * inv).reshape(n_tokens, dim // 2)


def apply_rope(x, ang):
    xf = x.astype(jnp.float32).reshape(*x.shape[:-1], -1, 2)
    cos, sin = jnp.cos(ang), jnp.sin(ang)
    x0, x1 = xf[..., 0], xf[..., 1]
    y = jnp.stack([x0 * cos - x1 * sin, x0 * sin + x1 * cos], axis=-1)
    return y.reshape(x.shape).astype(x.dtype)


def neighbourhood_attention(q, k, v, kc, vc, rpb):
    B, H, T, d = q.shape
    rows = T // GRID_W
    kh, kw = min(NA_WIN_ROWS, rows), NA_WIN_COLS
    scale = d ** -0.5
    qg = q.reshape(B, H, rows, GRID_W, d)
    kg = k.reshape(B, H, rows, GRID_W, d)
    vg = v.reshape(B, H, rows, GRID_W, d)
    col = np.arange(GRID_W)
    col_start = np.clip(col - kw // 2, 0, GRID_W - kw)
    col_idx = col_start[:, None] + np.arange(kw)
    dx_idx = col_idx - col[:, None] + (NA_WIN_COLS - 1)
    rpb_cols = rpb[:, :, dx_idx]

    def one_row(r):
        r0 = jnp.clip(r - kh // 2, 0, rows - kh)
        q_r = lax.dynamic_index_in_dim(qg, r, axis=2, keepdims=False)
        k_w = lax.dynamic_slice_in_dim(kg, r0, kh, axis=2)[:, :, :, col_idx]
        v_w = lax.dynamic_slice_in_dim(vg, r0, kh, axis=2)[:, :, :, col_idx]
        dy_idx = r0 + jnp.arange(kh) - r + (NA_WIN_ROWS - 1)
        bias = jnp.take(rpb_cols, dy_idx, axis=1).transpose(0, 2, 1, 3)
        s_loc = jnp.einsum('bhjd,bhajkd->bhjak', q_r, k_w) * scale + bias[None]
        s_ctx = jnp.einsum('bhjd,bhld->bhjl', q_r, kc) * scale
        s = jnp.concatenate([s_loc.reshape(B, H, GRID_W, kh * kw), s_ctx], axis=-1)
        p = softmax_f32(s, v.dtype)
        p_loc = p[..., :kh * kw].reshape(B, H, GRID_W, kh, kw)
        return (jnp.einsum('bhjak,bhajkd->bhjd', p_loc, v_w)
                + jnp.einsum('bhjl,bhld->bhjd', p[..., kh * kw:], vc))

    out = lax.map(one_row, jnp.arange(rows))
    return out.transpose(1, 2, 0, 3, 4).reshape(B, H, T, d)


def token_shift(u, mu_prev, mu_next):
    prev = jnp.pad(u[:, :-1], ((0, 0), (1, 0), (0, 0)))
    nxt = jnp.pad(u[:, 1:], ((0, 0), (0, 1), (0, 0)))
    return u + mu_prev * (prev - u) + mu_next * (nxt - u)


def rwkv7_terms(u, w0, w2, a0, a2, k_k, k_a):
    B, T, _ = u.shape
    W, R = RW_WIDTH, RW_DECAY_RANK
    r, k, v, wd, ad = jnp.split(u.astype(jnp.float32), [W, 2 * W, 3 * W, 3 * W + 2 * R], axis=-1)
    wd = wd.reshape(B, T, 2, R)
    ad = ad.reshape(B, T, 2, RW_ICLR_RANK)
    w = w0 + jnp.einsum('btdr,drc->btdc', jnp.tanh(wd), w2)
    decay = jnp.exp(-jnp.exp(-jax.nn.softplus(-w) - 0.5))
    a = jax.nn.sigmoid(a0 + jnp.einsum('btdr,drc->btdc', ad, a2))
    kk = (k * k_k).reshape(B, T, RW_HEADS, RW_HEAD_DIM)
    kk = kk / jnp.maximum(jnp.linalg.norm(kk, axis=-1, keepdims=True), 1e-12)
    k_dir = k[:, :, None] * (1.0 + (a - 1.0) * k_a)

    def hd(t):
        return t.reshape(*t.shape[:-1], RW_HEADS, RW_HEAD_DIM)

    return hd(r), hd(decay), hd(k_dir), hd(v), kk, hd(a)


def dir_stack(t_fwd, t_bwd):
    return jnp.moveaxis(jnp.stack([t_fwd, jnp.flip(t_bwd, 1)], axis=0), 2, 0)


def dir_merge(y):
    y = jnp.moveaxis(y, 0, 2)
    return y[0] + jnp.flip(y[1], 1)


def rwkv7_scan(terms, s0):
    r, decay, k_dir, v, kk, a = terms
    xs = (dir_stack(r, r), dir_stack(decay[:, :, 0], decay[:, :, 1]),
          dir_stack(k_dir[:, :, 0], k_dir[:, :, 1]), dir_stack(v, v),
          dir_stack(-kk, -kk), dir_stack(kk * a[:, :, 0], kk * a[:, :, 1]))

    def step(S, inp):
        r_t, w_t, k_t, v_t, a_t, b_t = inp
        sa = jnp.einsum('...vk,...k->...v', S, a_t)
        S = S * w_t[..., None, :] + sa[..., :, None] * b_t[..., None, :] + v_t[..., :, None] * k_t[..., None, :]
        return S, jnp.einsum('...vk,...k->...v', S, r_t)

    s_fin, y = lax.scan(step, s0, xs)
    return dir_merge(y), s_fin


def rwkv7_readout(y, terms, r_k, ln_w, ln_b, dtype):
    r, _, k_dir, v, _, _ = terms
    B, T = y.shape[:2]
    mu = jnp.mean(y, axis=-1, keepdims=True)
    var = jnp.mean(jnp.square(y - mu), axis=-1, keepdims=True)
    yn = ((y - mu) * lax.rsqrt(var + RW_GN_EPS)).reshape(B, T, RW_WIDTH) * ln_w + ln_b
    bonus = jnp.sum(r[:, :, None] * k_dir * r_k, axis=(2, 4))[..., None] * v
    return (yn + bonus.reshape(B, T, RW_WIDTH)).astype(dtype)


def rwkv7_mixer(u, uc, w0, w2, a0, a2, k_k, k_a, r_k, ln_w, ln_b, ctx_out):
    terms = rwkv7_terms(u, w0, w2, a0, a2, k_k, k_a)
    terms_c = rwkv7_terms(uc, w0, w2, a0, a2, k_k, k_a)
    s0 = jnp.zeros((2, u.shape[0], RW_HEADS, RW_HEAD_DIM, RW_HEAD_DIM), jnp.float32)
    yc, s_ctx = rwkv7_scan(terms_c, s0)
    y, _ = rwkv7_scan(terms, s_ctx)
    out = rwkv7_readout(y, terms, r_k, ln_w, ln_b, u.dtype)
    if not ctx_out:
        return out, None
    return out, rwkv7_readout(yc, terms_c, r_k, ln_w, ln_b, u.dtype)


def mla_queries(cq, q_norm, w_uq):
    q = heads(rms_norm(cq, q_norm) @ w_uq, MLA_HEADS)
    return q[..., :MLA_NOPE_DIM], q[..., MLA_NOPE_DIM:]


def mla_keys(ckv, kpe, kv_norm, w_ukv):
    kv = heads(rms_norm(ckv, kv_norm) @ w_ukv, MLA_HEADS)
    return kv[..., :MLA_NOPE_DIM], kpe, kv[..., MLA_NOPE_DIM:]


def mla_attention(q_nope, q_pe, k_nope, k_pe, v):
    B, H, T, _ = q_nope.shape
    nb = T // Q_BLOCK
    scale = (MLA_NOPE_DIM + MLA_ROPE_DIM) ** -0.5
    qn = jnp.moveaxis(q_nope.reshape(B, H, nb, Q_BLOCK, -1), 2, 0)
    qp = jnp.moveaxis(q_pe.reshape(B, H, nb, Q_BLOCK, -1), 2, 0)

    def block(args):
        qn_b, qp_b = args
        s = (jnp.einsum('bhqd,bhkd->bhqk', qn_b, k_nope) + jnp.einsum('bhqd,bkd->bhqk', qp_b, k_pe)) * scale
        return jnp.einsum('bhqk,bhkd->bhqd', softmax_f32(s, v.dtype), v)

    out = lax.map(block, (qn, qp))
    return jnp.moveaxis(out, 0, 2).reshape(B, H, T, -1)


def hgrn2_chunk_scan(q, log_f, k, v, s0):
    *lead, T, _ = q.shape
    dv = v.shape[-1]
    n = T // HG_CHUNK

    def chunks(t):
        return t.reshape(*lead, n, HG_CHUNK, t.shape[-1])

    qc, gc, kc, vc = chunks(q), chunks(log_f), chunks(k), chunks(v)
    b = jnp.cumsum(gc, axis=-2)
    b_last = b[..., -1:, :]
    q_in = qc * jnp.exp(b)
    k_in = kc * jnp.exp(-b)
    k_end = kc * jnp.exp(b_last - b)
    mask = jnp.tril(jnp.ones((HG_CHUNK, HG_CHUNK), jnp.float32))
    attn = jnp.einsum('...id,...jd->...ij', q_in, k_in) * mask
    o_intra = jnp.einsum('...ij,...je->...ie', attn, vc)
    ds = jnp.einsum('...cd,...ce->...de', k_end, vc)
    dec = jnp.exp(b_last[..., 0, :])

    def step(S, inp):
        d_c, ds_c = inp
        return d_c[..., :, None] * S + ds_c, S

    s_fin, s_prev = lax.scan(step, s0, (jnp.moveaxis(dec, -2, 0), jnp.moveaxis(ds, -3, 0)))
    s_prev = jnp.moveaxis(s_prev, 0, -3)
    o = o_intra + jnp.einsum('...cd,...de->...ce', q_in, s_prev)
    return o.reshape(*lead, T, dv), s_fin


def hgrn2_terms(zz, lb):
    F = HG_FDIM
    B, T, _ = zz.shape
    q, f_fw, f_bw, i = jnp.split(zz, [F, 2 * F, 3 * F], axis=-1)
    fz = jnp.stack([f_fw, f_bw], axis=0).astype(jnp.float32)
    log_f = jnp.logaddexp(jnp.log(lb), jnp.log1p(-lb) + jax.nn.log_sigmoid(fz))
    k = (1.0 - lb) * jax.nn.sigmoid(-fz)

    def dir_heads(t):
        t = t.reshape(2, B, T, HG_HEADS, -1).transpose(0, 1, 3, 2, 4)
        return jnp.stack([t[0], jnp.flip(t[1], 2)], axis=0)

    def shared(t):
        t = heads(t, HG_HEADS).astype(jnp.float32)
        return jnp.stack([t, jnp.flip(t, 2)], axis=0)

    return shared(q), dir_heads(log_f), dir_heads(k), shared(i)


def hgrn2_mixer(zz, zzc, lb, norm_w, ctx_out):
    s0 = jnp.zeros((2, zz.shape[0], HG_HEADS, HG_EXPAND, HG_HEAD_I), jnp.float32)
    oc, s_ctx = hgrn2_chunk_scan(*hgrn2_terms(zzc, lb), s0)
    ol, _ = hgrn2_chunk_scan(*hgrn2_terms(zz, lb), s_ctx)

    def readout(o):
        o = o[0] + jnp.flip(o[1], 2)
        return merge(rms_norm(o, norm_w)).astype(zz.dtype)

    if not ctx_out:
        return readout(ol), None
    return readout(ol), readout(oc)


def even_mixers(z, zc, rpb, mu, w0, w2, a0, a2, k_k, k_a, r_k, ln_w, ln_b, ctx_out):
    s1 = 3 * NA_WIDTH
    s2 = s1 + RW_SHIFT_COLS
    q, k, v = (heads(t, NA_HEADS) for t in jnp.split(z[..., :s1], 3, axis=-1))
    qc_raw, kc_raw, vc_raw = jnp.split(zc[..., :s1], 3, axis=-1)
    kc, vc = heads(kc_raw, NA_HEADS), heads(vc_raw, NA_HEADS)
    y_na = merge(neighbourhood_attention(q, k, v, kc, vc, rpb))
    u = token_shift(z[..., s1:s2], mu[0], mu[1])
    uc = token_shift(zc[..., s1:s2], mu[0], mu[1])
    y_rw, yc_rw = rwkv7_mixer(u, uc, w0, w2, a0, a2, k_k, k_a, r_k, ln_w, ln_b, ctx_out)
    y = jnp.concatenate([y_na, y_rw], axis=-1) * jax.nn.silu(z[..., s2:])
    if not ctx_out:
        return y, None
    yc_na = merge(dense_attention(heads(qc_raw, NA_HEADS), kc, vc))
    yc = jnp.concatenate([yc_na, yc_rw], axis=-1) * jax.nn.silu(zc[..., s2:])
    return y, yc


def odd_mixers(z, zc, q_norm, w_uq, kv_norm, w_ukv, lb, hg_norm_w, ctx_out):
    o1 = MLA_Q_RANK
    o2 = o1 + MLA_KV_RANK
    o3 = o2 + MLA_ROPE_DIM
    o4 = o3 + 3 * HG_FDIM + HG_WIDTH
    ang = axial_rope_angles(z.shape[1], MLA_ROPE_DIM)
    qn, qp = mla_queries(z[..., :o1], q_norm, w_uq)
    kn, kp, v = mla_keys(z[..., o1:o2], z[..., o2:o3], kv_norm, w_ukv)
    knc, kpc, vc = mla_keys(zc[..., o1:o2], zc[..., o2:o3], kv_norm, w_ukv)
    qp = apply_rope(qp, ang)
    kp = apply_rope(kp, ang)
    y_mla = merge(mla_attention(qn, qp, jnp.concatenate([kn, knc], axis=2),
                                jnp.concatenate([kp, kpc], axis=1), jnp.concatenate([v, vc], axis=2)))
    y_hg, yc_hg = hgrn2_mixer(z[..., o3:o4], zc[..., o3:o4], lb, hg_norm_w, ctx_out)
    y = jnp.concatenate([y_mla, y_hg], axis=-1) * jax.nn.silu(z[..., o4:])
    if not ctx_out:
        return y, None
    qnc, qpc = mla_queries(zc[..., :o1], q_norm, w_uq)
    yc_mla = merge(mla_attention(qnc, qpc, knc, kpc, vc))
    yc = jnp.concatenate([yc_mla, yc_hg], axis=-1) * jax.nn.silu(zc[..., o4:])
    return y, yc


def setup_inputs(seed: int = 0) -> dict:
    key = jax.random.key(seed)
    ks = iter(jax.random.split(key, 40))
    f32 = jnp.float32
    D = D_MODEL
    ne = (DEPTH + 1) // 2
    no = DEPTH // 2

    def nrm(shape, scale):
        return jax.random.normal(next(ks), shape, f32) * scale

    def unif(shape, lo, hi):
        return jax.random.uniform(next(ks), shape, f32, lo, hi)

    return {
        'x': nrm((BATCH, SEQ, D), 1.0),
        'c': nrm((BATCH, D), 1.0),
        'ctx': nrm((BATCH, CTX_LEN, D), 1.0),
        'c_ctx': nrm((D,), 1.0),
        'ada_w': nrm((DEPTH, D, 3 * D), 0.5 * D ** -0.5),
        'ada_b': nrm((DEPTH, 3 * D), 0.02),
        'norm_w': 1.0 + nrm((DEPTH, D), 0.02),
        'e_w_in': nrm((ne, D, EVEN_IN), D ** -0.5),
        'e_w_out': nrm((ne, EVEN_MIX, D), EVEN_MIX ** -0.5),
        'na_rpb': nrm((ne, NA_HEADS, 2 * NA_WIN_ROWS - 1, 2 * NA_WIN_COLS - 1), 0.2),
        'rw_mu': unif((ne, 2, RW_SHIFT_COLS), 0.0, 0.5),
        'rw_w0': unif((ne, 2, RW_WIDTH), -6.0, 1.0),
        'rw_w2': nrm((ne, 2, RW_DECAY_RANK, RW_WIDTH), 0.1),
        'rw_a0': nrm((ne, 2, RW_WIDTH), 0.5),
        'rw_a2': nrm((ne, 2, RW_ICLR_RANK, RW_WIDTH), 0.1),
        'rw_k_k': 0.85 + nrm((ne, RW_WIDTH), 0.05),
        'rw_k_a': 1.0 + nrm((ne, RW_WIDTH), 0.05),
        'rw_r_k': nrm((ne, RW_HEADS, RW_HEAD_DIM), 0.1),
        'rw_ln_w': 1.0 + nrm((ne, RW_WIDTH), 0.02),
        'rw_ln_b': nrm((ne, RW_WIDTH), 0.02),
        'o_w_in': nrm((no, D, ODD_IN), D ** -0.5),
        'o_w_out': nrm((no, ODD_MIX, D), ODD_MIX ** -0.5),
        'mla_q_norm': 1.0 + nrm((no, MLA_Q_RANK), 0.02),
        'mla_w_uq': nrm((no, MLA_Q_RANK, MLA_HEADS * (MLA_NOPE_DIM + MLA_ROPE_DIM)), MLA_Q_RANK ** -0.5),
        'mla_kv_norm': 1.0 + nrm((no, MLA_KV_RANK), 0.02),
        'mla_w_ukv': nrm((no, MLA_KV_RANK, MLA_HEADS * (MLA_NOPE_DIM + MLA_V_DIM)), MLA_KV_RANK ** -0.5),
        'hg_lower_bounds': 1.0 + nrm((DEPTH, HG_FDIM), 0.1),
        'hg_norm_w': 1.0 + nrm((no, HG_HEAD_I), 0.02),
        'final_norm_w': 1.0 + nrm((D,), 0.02),
    }


def reference(x, c, ctx, c_ctx, ada_w, ada_b, norm_w, e_w_in, e_w_out, na_rpb, rw_mu, rw_w0, rw_w2,
              rw_a0, rw_a2, rw_k_k, rw_k_a, rw_r_k, rw_ln_w, rw_ln_b, o_w_in, o_w_out, mla_q_norm,
              mla_w_uq, mla_kv_norm, mla_w_ukv, hg_lower_bounds, hg_norm_w, final_norm_w):
    s = jax.nn.softmax(hg_lower_bounds.astype(jnp.float32), axis=0)
    lower_bounds = jnp.cumsum(s, axis=0) - s[0]
    xc = ctx
    for l in range(DEPTH):
        ctx_out = l < DEPTH - 1
        i = l // 2
        shift, scale, gate = adaln(c[:, None, :], ada_w[l], ada_b[l])
        shift_c, scale_c, gate_c = adaln(c_ctx, ada_w[l], ada_b[l])
        h = rms_norm(x, norm_w[l]) * (1.0 + scale) + shift
        hc = rms_norm(xc, norm_w[l]) * (1.0 + scale_c) + shift_c
        if l % 2 == 0:
            y, yc = even_mixers(h @ e_w_in[i], hc @ e_w_in[i], na_rpb[i], rw_mu[i], rw_w0[i], rw_w2[i],
                                rw_a0[i], rw_a2[i], rw_k_k[i], rw_k_a[i], rw_r_k[i], rw_ln_w[i],
                                rw_ln_b[i], ctx_out)
            w_out = e_w_out[i]
        else:
            y, yc = odd_mixers(h @ o_w_in[i], hc @ o_w_in[i], mla_q_norm[i], mla_w_uq[i], mla_kv_norm[i],
                               mla_w_ukv[i], lower_bounds[l], hg_norm_w[i], ctx_out)
            w_out = o_w_out[i]
        x = x + gate * (y @ w_out)
        if ctx_out:
            xc = xc + gate_c * (yc @ w_out)
    return rms_norm(x, final_norm_w)
```

```python
import numpy as np
from contextlib import ExitStack
import concourse.bass as bass
import concourse.mybir as mybir
from concourse.bass_utils import run_bass_kernel_spmd

F32 = mybir.dt.float32
F32R = mybir.dt.float32r
BF16 = mybir.dt.bfloat16
AF = mybir.ActivationFunctionType
ALU = mybir.AluOpType
AX = mybir.AxisListType


class Buf:
    __slots__ = ("name", "last_w", "readers")

    def __init__(self, name=""):
        self.name = name
        self.last_w = None
        self.readers = {}


class Sched:
    ENG = ("pe", "act", "dve", "pool", "sp")

    def __init__(self, nc, stack, n_dma_sems=12):
        self.nc = nc
        self.q = {e: [] for e in self.ENG}
        self.cnt = {}
        self.sem = {}
        self.seen = {e: {} for e in self.ENG}
        for e in ("pe", "act", "dve", "pool"):
            self.sem[e] = stack.enter_context(nc.semaphore("s_" + e))
            self.cnt[e] = 0
        self.dma_keys = {}
        self.dma_next = {}
        for qn in ("sp", "pool", "act"):
            keys = []
            for i in range(n_dma_sems):
                k = "d_%s_%d" % (qn, i)
                self.sem[k] = stack.enter_context(nc.semaphore(k))
                self.cnt[k] = 0
                keys.append(k)
            self.dma_keys[qn] = keys
            self.dma_next[qn] = 0
        self.ninst = 0

    def _wait(self, eng, key, val):
        if self.seen[eng].get(key, 0) >= val:
            return
        self.seen[eng][key] = val
        sem = self.sem[key]
        self.q[eng].append(lambda e, sem=sem, val=val: e.wait_ge(sem, val))

    def _deps(self, eng, reads, writes, skip_key=None):
        deps = {}
        for r in reads:
            if r.last_w is not None:
                k, v = r.last_w
                deps[k] = max(deps.get(k, 0), v)
        for w in writes:
            if w.last_w is not None:
                k, v = w.last_w
                deps[k] = max(deps.get(k, 0), v)
            for k, v in w.readers.items():
                deps[k] = max(deps.get(k, 0), v)
        for k, v in deps.items():
            if k == skip_key:
                continue
            self._wait(eng, k, v)

    def _mark(self, tok, reads, writes):
        k, v = tok
        for r in reads:
            r.readers[k] = max(r.readers.get(k, 0), v)
        for w in writes:
            w.last_w = tok
            w.readers = {}

    def op(self, eng, name, reads=(), writes=(), same_eng_sync=True, **kw):
        if eng == "pe":
            same_eng_sync = False
        self._deps(eng, reads, writes, skip_key=(None if same_eng_sync else eng))
        self.cnt[eng] += 1
        sem = self.sem[eng]
        self.q[eng].append(lambda e, name=name, kw=kw, sem=sem: getattr(e, name)(**kw).then_inc(sem, 1))
        self._mark((eng, self.cnt[eng]), reads, writes)
        self.ninst += 1

    def dma(self, qn, out, in_, reads=(), writes=(), **kw):
        keys = self.dma_keys[qn]
        i = self.dma_next[qn]
        self.dma_next[qn] = (i + 1) % len(keys)
        key = keys[i]
        if self.cnt[key] > 0:
            self._wait(qn, key, self.cnt[key])
        self._deps(qn, reads, writes)
        self.cnt[key] += 16
        sem = self.sem[key]
        self.q[qn].append(lambda e, out=out, in_=in_, sem=sem, kw=kw: e.dma_start(out=out, in_=in_, **kw).then_inc(sem, 16))
        self._mark((key, self.cnt[key]), reads, writes)
        self.ninst += 1

    def finish(self):
        for qn, keys in self.dma_keys.items():
            for k in keys:
                if self.cnt[k] > 0:
                    self._wait("sp", k, self.cnt[k])
        for e in ("pe", "act", "dve", "pool"):
            if self.cnt[e] > 0:
                self._wait("sp", e, self.cnt[e])

    def emit(self):
        nc = self.nc
        q = self.q
        with nc.Block() as block:
            @block.tensor
            def _(e):
                for f in q["pe"]:
                    f(e)

            @block.scalar
            def _(e):
                for f in q["act"]:
                    f(e)

            @block.vector
            def _(e):
                for f in q["dve"]:
                    f(e)

            @block.gpsimd
            def _(e):
                for f in q["pool"]:
                    f(e)

            @block.sync
            def _(e):
                for f in q["sp"]:
                    f(e)


class Ctx:
    def __init__(self, nc, stack):
        self.nc = nc
        self.stack = stack
        self.n = 0

    def sb(self, shape, dtype=F32, name=None):
        self.n += 1
        t = self.stack.enter_context(self.nc.sbuf_tensor(name or ("sb%d" % self.n), list(shape), dtype))
        return t

    def ps(self, shape, dtype=F32, name=None):
        self.n += 1
        t = self.stack.enter_context(self.nc.psum_tensor(name or ("ps%d" % self.n), list(shape), dtype))
        return t


def _mm(self, out, lhsT, rhs, start=True, stop=True, reads=(), writes=()):
    self.op("pe", "matmul", reads=reads, writes=writes, out=out, lhsT=lhsT, rhs=rhs, start=start, stop=stop)


def _act(self, out, in_, func, reads=(), writes=(), **kw):
    self.op("act", "activation", reads=reads, writes=writes, out=out, in_=in_, func=func, **kw)


def _stt(self, out, in0, scalar, in1, op0, op1, reads=(), writes=(), eng="dve"):
    self.op(eng, "scalar_tensor_tensor", reads=reads, writes=writes, out=out, in0=in0, scalar=scalar, in1=in1, op0=op0, op1=op1)


def _tt(self, out, in0, in1, op, reads=(), writes=(), eng="dve"):
    self.op(eng, "tensor_tensor", reads=reads, writes=writes, out=out, in0=in0, in1=in1, op=op)


def _ts(self, out, in0, s1, op0, s2=None, op1=None, reads=(), writes=(), eng="dve"):
    kw = dict(out=out, in0=in0, scalar1=s1, scalar2=s2, op0=op0)
    if op1 is not None:
        kw["op1"] = op1
    self.op(eng, "tensor_scalar", reads=reads, writes=writes, **kw)


def _cp(self, out, in_, reads=(), writes=(), eng="dve"):
    if eng == "act":
        self.op("act", "copy", reads=reads, writes=writes, out=out, in_=in_)
    else:
        self.op(eng, "tensor_copy", reads=reads, writes=writes, out=out, in_=in_)


Sched.mm = _mm
Sched.act = _act
Sched.stt = _stt
Sched.tt = _tt
Sched.ts = _ts
Sched.cp = _cp


TOK = 1152
TT = 384
NT = 3
D = 2048
KC = 16
EPS = 1e-6


def segs(t):
    lo, hi = t * TT, (t + 1) * TT
    out = []
    if lo < 1024:
        out.append((lo, min(hi, 1024), 0))
    if hi > 1024:
        out.append((max(lo, 1024), hi, 1))
    return out


def build_toklocal(n_in, has_out, final=False):
    has_in = n_in > 0
    nc = bass.Bass("TRN2", target_bir_lowering=False)
    NB = (16 if has_out else 0) + (32 if has_in else 0)
    nblk = (n_in + 127) // 128
    xT = nc.dram_tensor("xT", [D, TOK], F32, kind="ExternalInput").ap()
    csil = nc.dram_tensor("csil", [128, KC, 2], F32, kind="ExternalInput").ap()
    if NB:
        adaw = nc.dram_tensor("adaw", [NB, 128, KC, 128], F32, kind="ExternalInput").ap()
        adab = nc.dram_tensor("adab", [128, NB], F32, kind="ExternalInput").ap()
    normw = nc.dram_tensor("normw", [128, KC], F32, kind="ExternalInput").ap()
    if has_out:
        yT = nc.dram_tensor("yT", [D, TOK], F32, kind="ExternalInput").ap()
        wout = nc.dram_tensor("wout", [KC, 128, KC, 128], F32, kind="ExternalInput").ap()
        xnT = nc.dram_tensor("xnT", [D, TOK], F32, kind="ExternalOutput").ap()
    if has_in:
        win = nc.dram_tensor("win", [nblk, 128, KC, 128], F32, kind="ExternalInput").ap()
        zT = nc.dram_tensor("zT", [nblk * 128, TOK], F32, kind="ExternalOutput").ap()
    if final:
        oT = nc.dram_tensor("oT", [D, TOK], F32, kind="ExternalOutput").ap()

    with ExitStack() as st:
        c = Ctx(nc, st)
        s = Sched(nc, st)
        sc = c.sb([128, KC, 2]); b_sc = Buf()
        nw = c.sb([128, KC]); b_nw = Buf()
        s.dma("sp", sc[:], csil, writes=[b_sc])
        s.dma("sp", nw[:], normw, writes=[b_nw])
        ones = c.sb([128, 128]); b_ones = Buf()
        s.op("dve", "memset", writes=[b_ones], ap=ones[:], constant=1.0)
        epsb = c.sb([128, 1]); b_eps = Buf()
        s.op("dve", "memset", writes=[b_eps], ap=epsb[:], constant=EPS)
        s.act(sc[:], sc[:], AF.Silu, reads=[b_sc], writes=[b_sc])
        ps_small = c.ps([128, 512]); b_pss = Buf()
        wring = [c.sb([128, KC, 128], F32R, name="wr%d" % i) for i in range(4)]
        b_wring = [Buf() for _ in range(4)]
        wf = [c.sb([128, KC, 128], F32, name="wf%d" % i) for i in range(2)]
        b_wf = [Buf() for _ in range(2)]
        if NB:
            ab = c.sb([128, NB]); b_ab = Buf()
            s.dma("sp", ab[:], adab, writes=[b_ab])
            mod = c.sb([128, NB, 2]); b_mod = Buf()
            for j in range(NB):
                w = wf[j % 2]; bw = b_wf[j % 2]
                s.dma("sp", w[:], adaw[j], writes=[bw])
                pso = ps_small[:, 2 * (j % 8):2 * (j % 8) + 2]
                for k in range(KC):
                    s.mm(pso, w[:, k, :], sc[:, k, :], start=(k == 0), stop=(k == KC - 1), reads=[bw, b_sc], writes=[b_pss])
                s.ts(mod[:, j, :], pso, ab[:, j:j + 1], ALU.add, reads=[b_pss, b_ab], writes=[b_mod])
        acoef = c.sb([128, KC, 2]); bcoef = c.sb([128, KC, 2]); b_coef = Buf()
        if has_in:
            o = 16 if has_out else 0
            for m in range(2):
                s.stt(acoef[:, :, m], mod[:, o + 16:o + 32, m], 1.0, nw[:], ALU.add, ALU.mult, reads=[b_mod, b_nw], writes=[b_coef])
                s.cp(bcoef[:, :, m], mod[:, o:o + 16, m], reads=[b_mod], writes=[b_coef])
        else:
            for m in range(2):
                s.cp(acoef[:, :, m], nw[:], reads=[b_nw], writes=[b_coef])
            s.op("dve", "memset", writes=[b_coef], ap=bcoef[:], constant=0.0)

        hT = c.sb([128, KC, TOK], F32R if has_in else F32); b_h = [Buf() for _ in range(NT)]
        xt = c.sb([128, KC, TT]); b_xt = Buf()
        if has_out:
            yt = c.sb([128, KC, TT], F32R); b_yt = Buf()
        sq = [c.sb([128, TT], name="sq%d" % i) for i in range(2)]; b_sq = [Buf(), Buf()]
        rstd = c.sb([128, TT]); b_rstd = Buf()
        tmp = c.sb([128, TT]); b_tmp = Buf()
        pring = [c.ps([128, 512], name="pr%d" % i) for i in range(4)]
        b_pring = [Buf() for _ in range(4)]
        ps_ss = c.ps([128, 512]); b_pss2 = Buf()
        pi = 0
        wi = 0
        xT_v = xT.rearrange("(k p) t -> p k t", p=128)
        for t in range(NT):
            tl, th = t * TT, (t + 1) * TT
            s.dma("sp", xt[:], xT_v[:, :, tl:th], writes=[b_xt])
            if has_out:
                s.dma("pool", yt[:], yT.rearrange("(k p) t -> p k t", p=128)[:, :, tl:th], writes=[b_yt])
                for db in range(KC):
                    w = wring[wi % 4]; bw = b_wring[wi % 4]; wi += 1
                    s.dma("pool", w[:], wout[db], writes=[bw])
                    ps = pring[pi % 4]; bp = b_pring[pi % 4]; pi += 1
                    for k in range(KC):
                        s.mm(ps[:, :TT], w[:, k, :], yt[:, k, :], start=(k == 0), stop=(k == KC - 1), reads=[bw, b_yt], writes=[bp])
                    for (lo, hi, m) in segs(t):
                        s.stt(xt[:, db, lo - tl:hi - tl], ps[:, lo - tl:hi - tl], mod[:, db, m:m + 1], xt[:, db, lo - tl:hi - tl],
                              ALU.mult, ALU.add, reads=[bp, b_mod, b_xt], writes=[b_xt])
                s.dma("sp", xnT.rearrange("(k p) t -> p k t", p=128)[:, :, tl:th], xt[:], reads=[b_xt])
            if has_in or final:
                for k in range(KC):
                    s.act(sq[k % 2][:], xt[:, k, :], AF.Square, reads=[b_xt], writes=[b_sq[k % 2]])
                    s.mm(ps_ss[:, :TT], ones[:], sq[k % 2][:], start=(k == 0), stop=(k == KC - 1), reads=[b_ones, b_sq[k % 2]], writes=[b_pss2])
                s.act(tmp[:], ps_ss[:, :TT], AF.Sqrt, reads=[b_pss2, b_eps], writes=[b_tmp], bias=epsb[:], scale=1.0 / D)
                s.op("dve", "reciprocal", reads=[b_tmp], writes=[b_rstd], out=rstd[:], in_=tmp[:])
                for k in range(KC):
                    for (lo, hi, m) in segs(t):
                        s.stt(xt[:, k, lo - tl:hi - tl], xt[:, k, lo - tl:hi - tl], acoef[:, k, m:m + 1], rstd[:, lo - tl:hi - tl],
                              ALU.mult, ALU.mult, reads=[b_xt, b_coef, b_rstd], writes=[b_xt])
                        s.act(hT[:, k, lo:hi], xt[:, k, lo - tl:hi - tl], AF.Identity, reads=[b_xt, b_coef], writes=[b_h[t]],
                              bias=bcoef[:, k, m:m + 1], scale=1.0)
                if final:
                    s.dma("sp", oT.rearrange("(k p) t -> p k t", p=128)[:, :, tl:th], hT[:, :, tl:th], reads=[b_h[t]])
        if has_in:
            zs = [c.sb([128, TT], name="zs%d" % i) for i in range(4)]
            b_zs = [Buf() for _ in range(4)]
            zi = 0
            for j in range(nblk):
                w = wring[wi % 4]; bw = b_wring[wi % 4]; wi += 1
                s.dma("pool", w[:], win[j], writes=[bw])
                for t in range(NT):
                    ps = pring[pi % 4]; bp = b_pring[pi % 4]; pi += 1
                    for k in range(KC):
                        s.mm(ps[:, :TT], w[:, k, :], hT[:, k, t * TT:(t + 1) * TT], start=(k == 0), stop=(k == KC - 1),
                             reads=[bw, b_h[t]], writes=[bp])
                    z = zs[zi % 4]; bz = b_zs[zi % 4]; zi += 1
                    s.cp(z[:], ps[:, :TT], reads=[bp], writes=[bz], eng=("act" if zi % 2 else "dve"))
                    s.dma("sp", zT[j * 128:(j + 1) * 128, t * TT:(t + 1) * TT], z[:], reads=[bz])
        s.finish()
        s.emit()
    return nc


class AttnCore:
    def __init__(self, s, c, QW):
        self.s, self.c, self.QW = s, c, QW
        self.ps_s = [c.ps([128, 512], name="pss%d" % i) for i in range(3)]
        self.b_ps_s = [Buf() for _ in range(3)]
        self.ps_o = [c.ps([128, 512], name="pso%d" % i) for i in range(2)]
        self.b_ps_o = [Buf() for _ in range(2)]
        self.ps_d = [c.ps([128, 512], name="psd%d" % i) for i in range(2)]
        self.b_ps_d = [Buf() for _ in range(2)]
        self.pT = [c.sb([128, QW], F32R, name="pT%d" % i) for i in range(3)]
        self.b_pT = [Buf() for _ in range(3)]
        self.tmp = [c.sb([128, QW], F32, name="atmp%d" % i) for i in range(2)]
        self.b_tmp = [Buf() for _ in range(2)]
        self.rden = c.sb([128, QW]); self.b_rden = Buf()
        self.yo = [c.sb([128, QW], name="yo%d" % i) for i in range(2)]
        self.b_yo = [Buf() for _ in range(2)]
        self.ones = c.sb([128, 128], F32R); self.b_ones = Buf()
        self.onesf = c.sb([128, 128], F32); self.b_onesf = Buf()
        s.op("dve", "memset", writes=[self.b_onesf], ap=self.onesf[:], constant=1.0)
        s.cp(self.ones[:], self.onesf[:], reads=[self.b_onesf], writes=[self.b_ones])
        self.si = 0
        self.oi = 0

    def tile(self, qparts, kblocks, scale, gate_ap, gate_buf, out_dram):
        s, QW = self.s, self.QW
        po = self.ps_o[self.oi % 2]; bpo = self.b_ps_o[self.oi % 2]
        pd = self.ps_d[self.oi % 2]; bpd = self.b_ps_d[self.oi % 2]
        yo = self.yo[self.oi % 2]; byo = self.b_yo[self.oi % 2]
        self.oi += 1
        n = len(kblocks)
        for j, kb in enumerate(kblocks):
            i3 = self.si % 3
            ps = self.ps_s[i3]; bps = self.b_ps_s[i3]
            pT = self.pT[i3]; bpT = self.b_pT[i3]
            tmp = self.tmp[self.si % 2]; btmp = self.b_tmp[self.si % 2]
            self.si += 1
            np_ = len(qparts)
            for a, ((q_ap, qb), (k_ap, kbuf)) in enumerate(zip(qparts, kb["k"])):
                s.mm(ps[:, :QW], k_ap, q_ap, start=(a == 0), stop=(a == np_ - 1), reads=list(qb) + list(kbuf), writes=[bps])
            if kb.get("bias") is not None:
                b_ap, bb = kb["bias"]
                s.stt(tmp[:], ps[:, :QW], float(scale), b_ap, ALU.mult, ALU.add, reads=[bps] + list(bb), writes=[btmp])
                s.act(pT[:], tmp[:], AF.Exp, reads=[btmp], writes=[bpT])
            else:
                s.act(pT[:], ps[:, :QW], AF.Exp, reads=[bps], writes=[bpT], scale=float(scale))
            v_ap, vb = kb["v"]
            s.mm(po[:, :QW], v_ap, pT[:], start=(j == 0), stop=(j == n - 1), reads=list(vb) + [bpT], writes=[bpo])
            s.mm(pd[:, :QW], self.ones[:], pT[:], start=(j == 0), stop=(j == n - 1), reads=[self.b_ones, bpT], writes=[bpd])
        s.op("dve", "reciprocal", reads=[bpd], writes=[self.b_rden], out=self.rden[:], in_=pd[:, :QW])
        s.tt(yo[:], po[:, :QW], self.rden[:], ALU.mult, reads=[bpo, self.b_rden], writes=[byo])
        if gate_ap is not None:
            s.tt(yo[:], yo[:], gate_ap, ALU.mult, reads=[byo] + list(gate_buf), writes=[byo])
        s.dma("sp", out_dram, yo[:], reads=[byo])


NA_SCALE = 128 ** -0.5


def na_kblocks(i):
    lo = min(max(4 * i - 4, 0), 24)
    hi = min(max(4 * i - 1, 0), 24) + 7
    return list(range(lo // 2, hi // 2 + 1))


def build_na():
    nc = bass.Bass("TRN2", target_bir_lowering=False)
    H = 4
    qT = nc.dram_tensor("qT", [H, 128, 2048], F32, kind="ExternalInput").ap()
    kT = nc.dram_tensor("kT", [H, 128, 2304], F32, kind="ExternalInput").ap()
    v = nc.dram_tensor("v", [H, 128, 18, 128], F32, kind="ExternalInput").ap()
    gT = nc.dram_tensor("gT", [H, 128, 2048], F32, kind="ExternalInput").ap()
    bias = nc.dram_tensor("bias", [H, 8, 128, 6, 256], F32, kind="ExternalInput").ap()
    qcT = nc.dram_tensor("qcT", [H, 128, 256], F32, kind="ExternalInput").ap()
    gcT = nc.dram_tensor("gcT", [H, 128, 256], F32, kind="ExternalInput").ap()
    yT = nc.dram_tensor("yT", [H, 128, 2048], F32, kind="ExternalOutput").ap()
    ycT = nc.dram_tensor("ycT", [H, 128, 256], F32, kind="ExternalOutput").ap()
    with ExitStack() as st:
        c = Ctx(nc, st)
        s = Sched(nc, st)
        core = AttnCore(s, c, 256)
        q_s = [c.sb([128, 2048], F32R, name="q%d" % i) for i in range(2)]
        k_s = [c.sb([128, 2304], F32R, name="k%d" % i) for i in range(2)]
        v_s = [c.sb([128, 18, 128], F32R, name="v%d" % i) for i in range(2)]
        g_s = [c.sb([128, 2048], F32, name="g%d" % i) for i in range(2)]
        qc_s = [c.sb([128, 256], F32R, name="qc%d" % i) for i in range(2)]
        gc_s = [c.sb([128, 256], F32, name="gc%d" % i) for i in range(2)]
        b_s = [c.sb([128, 6, 256], F32, name="b%d" % i) for i in range(2)]
        bq = [Buf() for _ in range(2)]; bk = [Buf() for _ in range(2)]; bv = [Buf() for _ in range(2)]
        bg = [Buf() for _ in range(2)]; bqc = [Buf() for _ in range(2)]; bgc = [Buf() for _ in range(2)]
        bb = [Buf() for _ in range(2)]
        bi = 0
        for h in range(H):
            p = h % 2
            s.dma("pool", q_s[p][:], qT[h], writes=[bq[p]])
            s.dma("pool", k_s[p][:], kT[h], writes=[bk[p]])
            s.dma("pool", v_s[p][:], v[h], writes=[bv[p]])
            s.dma("pool", qc_s[p][:], qcT[h], writes=[bqc[p]])
            s.dma("sp", g_s[p][:], gT[h], writes=[bg[p]])
            s.dma("sp", gc_s[p][:], gcT[h], writes=[bgc[p]])
            s.act(g_s[p][:], g_s[p][:], AF.Silu, reads=[bg[p]], writes=[bg[p]])
            s.act(gc_s[p][:], gc_s[p][:], AF.Silu, reads=[bgc[p]], writes=[bgc[p]])
            for i in range(8):
                bt = b_s[bi % 2]; bbt = bb[bi % 2]; bi += 1
                s.dma("sp", bt[:], bias[h, i], writes=[bbt])
                kbs = []
                for j, kb in enumerate(na_kblocks(i)):
                    kbs.append(dict(k=[(k_s[p][:, kb * 128:(kb + 1) * 128], [bk[p]])], v=(v_s[p][:, kb, :], [bv[p]]), bias=(bt[:, j, :], [bbt])))
                for kb in (16, 17):
                    kbs.append(dict(k=[(k_s[p][:, kb * 128:(kb + 1) * 128], [bk[p]])], v=(v_s[p][:, kb, :], [bv[p]]), bias=None))
                core.tile([(q_s[p][:, i * 256:(i + 1) * 256], [bq[p]])], kbs, NA_SCALE, g_s[p][:, i * 256:(i + 1) * 256], [bg[p]],
                          yT[h, :, i * 256:(i + 1) * 256])
            kbs = [dict(k=[(k_s[p][:, kb * 128:(kb + 1) * 128], [bk[p]])], v=(v_s[p][:, kb, :], [bv[p]]), bias=None) for kb in (16, 17)]
            core.tile([(qc_s[p][:], [bqc[p]])], kbs, NA_SCALE, gc_s[p][:], [bgc[p]], ycT[h])
        s.finish()
        s.emit()
    return nc


def na_bias_host(rpb):
    out = np.full((8, 8, 128, 6, 256), -30000.0, np.float32)
    qf = np.arange(256)
    kp = np.arange(128)
    for i in range(8):
        qt = i * 256 + qf
        rq, cq = qt // 64, qt % 64
        r0 = np.clip(rq - 4, 0, 24)
        c0 = np.clip(cq - 8, 0, 48)
        for j, kb in enumerate(na_kblocks(i)):
            kt = kb * 128 + kp
            rk, ck = kt // 64, kt % 64
            valid = ((rk[:, None] >= r0[None, :]) & (rk[:, None] < r0[None, :] + 8) &
                     (ck[:, None] >= c0[None, :]) & (ck[:, None] < c0[None, :] + 16))
            dy = np.clip(rk[:, None] - rq[None, :] + 7, 0, 14)
            dx = np.clip(ck[:, None] - cq[None, :] + 15, 0, 30)
            vals = rpb[:, dy, dx]
            out[:, i, :, j, :] = np.where(valid[None], vals, np.float32(-30000.0))
    return out


MLA_SCALE = 192 ** -0.5


def build_mla():
    nc = bass.Bass("TRN2", target_bir_lowering=False)
    H = 4
    T, TK = 2048, 2304
    cqT = nc.dram_tensor("cqT", [128, 4, T], F32, kind="ExternalInput").ap()
    ckvT = nc.dram_tensor("ckvT", [128, 4, TK], F32, kind="ExternalInput").ap()
    kpeT = nc.dram_tensor("kpeT", [64, TK], F32, kind="ExternalInput").ap()
    kpeS = nc.dram_tensor("kpeS", [64, T], F32, kind="ExternalInput").ap()
    cosT = nc.dram_tensor("cosT", [64, T], F32, kind="ExternalInput").ap()
    sinT = nc.dram_tensor("sinT", [64, T], F32, kind="ExternalInput").ap()
    qnorm = nc.dram_tensor("qnorm", [128, 4], F32, kind="ExternalInput").ap()
    kvnorm = nc.dram_tensor("kvnorm", [128, 4], F32, kind="ExternalInput").ap()
    wqn = nc.dram_tensor("wqn", [H, 128, 4, 128], F32, kind="ExternalInput").ap()
    wqp = nc.dram_tensor("wqp", [H, 128, 4, 64], F32, kind="ExternalInput").ap()
    wqs = nc.dram_tensor("wqs", [H, 128, 4, 64], F32, kind="ExternalInput").ap()
    wkn = nc.dram_tensor("wkn", [H, 128, 4, 128], F32, kind="ExternalInput").ap()
    wv = nc.dram_tensor("wv", [128, 4, 512], F32, kind="ExternalInput").ap()
    gT = nc.dram_tensor("gT", [H, 128, T], F32, kind="ExternalInput").ap()
    yT = nc.dram_tensor("yT", [H, 128, T], F32, kind="ExternalOutput").ap()
    with ExitStack() as st:
        c = Ctx(nc, st)
        s = Sched(nc, st)
        core = AttnCore(s, c, 512)
        onesf = c.sb([128, 128]); b_onesf = Buf()
        s.op("dve", "memset", writes=[b_onesf], ap=onesf[:], constant=1.0)
        epsb = c.sb([128, 1]); b_eps = Buf()
        s.op("dve", "memset", writes=[b_eps], ap=epsb[:], constant=1e-6)
        nrm = c.sb([128, 8]); b_nrm = Buf()
        s.dma("sp", nrm[:, 0:4], qnorm, writes=[b_nrm])
        s.dma("sp", nrm[:, 4:8], kvnorm, writes=[b_nrm])
        cn = c.sb([128, 4, T], F32R); b_cn = Buf()
        kvn = c.sb([128, 4, TK], F32R); b_kvn = Buf()
        stage = [c.sb([128, 4, 512], name="stg0")] * 2; b_stage = [Buf()] * 2
        sq = [c.sb([128, 512], name="msq%d" % i) for i in range(2)]; b_sq = [Buf(), Buf()]
        rstd = c.sb([128, 512]); b_rstd = Buf()
        pp = core.ps_s
        bpp = core.b_ps_s
        pi = 0
        si = 0
        for (src, dst, bdst, ncol, noff) in ((cqT, cn, b_cn, T, 0), (ckvT, kvn, b_kvn, TK, 4)):
            t0 = 0
            while t0 < ncol:
                w = min(512, ncol - t0)
                stg = stage[si % 2]; bst = b_stage[si % 2]; si += 1
                s.dma("sp", stg[:, :, :w], src[:, :, t0:t0 + w], writes=[bst])
                ps = pp[pi % 3]; bps = bpp[pi % 3]; pi += 1
                for k in range(4):
                    s.act(sq[k % 2][:, :w], stg[:, k, :w], AF.Square, reads=[bst], writes=[b_sq[k % 2]])
                    s.mm(ps[:, :w], onesf[:], sq[k % 2][:, :w], start=(k == 0), stop=(k == 3), reads=[b_onesf, b_sq[k % 2]], writes=[bps])
                s.act(rstd[:, :w], ps[:, :w], AF.Sqrt, reads=[bps, b_eps], writes=[b_rstd], bias=epsb[:], scale=1.0 / 512)
                s.op("dve", "reciprocal", reads=[b_rstd], writes=[b_rstd], out=rstd[:, :w], in_=rstd[:, :w])
                for k in range(4):
                    s.stt(dst[:, k, t0:t0 + w], stg[:, k, :w], nrm[:, noff + k:noff + k + 1], rstd[:, :w], ALU.mult, ALU.mult,
                          reads=[bst, b_nrm, b_rstd], writes=[bdst])
                t0 += w
        wv_s = c.sb([128, 4, 512], F32R); b_wv = Buf()
        s.dma("pool", wv_s[:], wv, writes=[b_wv])
        v_h = c.sb([128, 18, 128], F32R); b_v = Buf()
        cos_s = c.sb([64, T]); sin_s = c.sb([64, T]); b_cs = Buf()
        s.dma("sp", cos_s[:], cosT, writes=[b_cs])
        s.dma("sp", sin_s[:], sinT, writes=[b_cs])
        kp = c.sb([64, TK], F32R); b_kp = Buf()
        stg = stage[0]; bst = b_stage[0]
        for t in range(4):
            tsl = slice(t * 512, (t + 1) * 512)
            s.dma("sp", stg[0:64, 0, :], kpeT[:, tsl], writes=[bst])
            s.dma("sp", stg[0:64, 1, :], kpeS[:, tsl], writes=[bst])
            s.tt(stg[0:64, 1, :], stg[0:64, 1, :], sin_s[:, tsl], ALU.mult, reads=[bst, b_cs], writes=[bst])
            s.tt(stg[0:64, 0, :], stg[0:64, 0, :], cos_s[:, tsl], ALU.mult, reads=[bst, b_cs], writes=[bst])
            s.tt(kp[:, tsl], stg[0:64, 0, :], stg[0:64, 1, :], ALU.add, reads=[bst], writes=[b_kp])
        s.dma("sp", stg[0:64, 0, 0:256], kpeT[:, T:TK], writes=[bst])
        s.cp(kp[:, T:], stg[0:64, 0, 0:256], reads=[bst], writes=[b_kp])
        wq_s = [c.sb([128, 4, 256], F32R, name="wq%d" % i) for i in range(2)]; b_wq = [Buf(), Buf()]
        wk_s = [c.sb([128, 4, 128], F32R, name="wk%d" % i) for i in range(2)]; b_wk = [Buf(), Buf()]
        qn = [c.sb([128, T], F32R, name="qn0")] * 2; b_qn = [Buf()] * 2
        qp = [c.sb([64, T], F32R, name="qp0")] * 2; b_qp = [Buf()] * 2
        kn = [c.sb([128, TK], F32R, name="kn0")] * 2; b_kn = [Buf()] * 2
        g_s = [c.sb([128, T], name="mg0")] * 2; b_g = [Buf()] * 2
        ra = c.sb([64, 512]); rb = c.sb([64, 512]); b_ra = Buf(); b_rb = Buf()
        for h in range(H):
            p = h % 2
            s.dma("pool", wq_s[p][:, :, 0:128], wqn[h], writes=[b_wq[p]])
            s.dma("pool", wq_s[p][:, :, 128:192], wqp[h], writes=[b_wq[p]])
            s.dma("pool", wq_s[p][:, :, 192:256], wqs[h], writes=[b_wq[p]])
            s.dma("pool", wk_s[p][:], wkn[h], writes=[b_wk[p]])
            s.dma("sp", g_s[p][:], gT[h], writes=[b_g[p]])
            s.act(g_s[p][:], g_s[p][:], AF.Silu, reads=[b_g[p]], writes=[b_g[p]])
            for t in range(4):
                tsl = slice(t * 512, (t + 1) * 512)
                ps = pp[pi % 3]; bps = bpp[pi % 3]; pi += 1
                for k in range(4):
                    s.mm(ps[:, :512], wq_s[p][:, k, 0:128], cn[:, k, tsl], start=(k == 0), stop=(k == 3), reads=[b_wq[p], b_cn], writes=[bps])
                s.cp(qn[p][:, tsl], ps[:, :512], reads=[bps], writes=[b_qn[p]], eng="act")
                ps = pp[pi % 3]; bps = bpp[pi % 3]; pi += 1
                for k in range(4):
                    s.mm(ps[0:64, :512], wq_s[p][:, k, 128:192], cn[:, k, tsl], start=(k == 0), stop=(k == 3), reads=[b_wq[p], b_cn], writes=[bps])
                s.tt(ra[:], ps[0:64, :512], cos_s[:, tsl], ALU.mult, reads=[bps, b_cs], writes=[b_ra])
                ps = pp[pi % 3]; bps = bpp[pi % 3]; pi += 1
                for k in range(4):
                    s.mm(ps[0:64, :512], wq_s[p][:, k, 192:256], cn[:, k, tsl], start=(k == 0), stop=(k == 3), reads=[b_wq[p], b_cn], writes=[bps])
                s.tt(rb[:], ps[0:64, :512], sin_s[:, tsl], ALU.mult, reads=[bps, b_cs], writes=[b_rb])
                s.tt(qp[p][:, tsl], ra[:], rb[:], ALU.add, reads=[b_ra, b_rb], writes=[b_qp[p]])
            for blk in range(18):
                ps = pp[pi % 3]; bps = bpp[pi % 3]; pi += 1
                for k in range(4):
                    s.mm(ps[:, :128], kvn[:, k, blk * 128:(blk + 1) * 128], wv_s[:, k, h * 128:(h + 1) * 128], start=(k == 0), stop=(k == 3),
                         reads=[b_kvn, b_wv], writes=[bps])
                s.cp(v_h[:, blk, :], ps[:, :128], reads=[bps], writes=[b_v], eng=("act" if blk % 2 else "dve"))
            t0 = 0
            while t0 < TK:
                w = min(512, TK - t0)
                ps = pp[pi % 3]; bps = bpp[pi % 3]; pi += 1
                for k in range(4):
                    s.mm(ps[:, :w], wk_s[p][:, k, :], kvn[:, k, t0:t0 + w], start=(k == 0), stop=(k == 3), reads=[b_wk[p], b_kvn], writes=[bps])
                s.cp(kn[p][:, t0:t0 + w], ps[:, :w], reads=[bps], writes=[b_kn[p]], eng="act")
                t0 += w
            for t in range(4):
                tsl = slice(t * 512, (t + 1) * 512)
                kbs = []
                for kb in range(18):
                    ksl = slice(kb * 128, (kb + 1) * 128)
                    kbs.append(dict(k=[(kn[p][:, ksl], [b_kn[p]]), (kp[:, ksl], [b_kp])],
                                    v=(v_h[:, kb, :], [b_v]), bias=None))
                core.tile([(qn[p][:, tsl], [b_qn[p]]), (qp[p][:, tsl], [b_qp[p]])], kbs, MLA_SCALE, g_s[p][:, tsl], [b_g[p]], yT[h, :, tsl])
        s.finish()
        s.emit()
    return nc


CH = 64


def chunk_masks():
    j = np.arange(128)[:, None]; i = np.arange(128)[None, :]
    same = (j // CH) == (i // CH)
    return (same & (j <= i)).astype(np.float32), (same & (j >= i)).astype(np.float32)


def build_hg():
    nc = bass.Bass("TRN2", target_bir_lowering=False)
    H, T, TK, NBLK, NCH = 4, 2048, 2304, 18, 36
    qT = nc.dram_tensor("qT", [H, 2, 128, TK], F32, kind="ExternalInput").ap()
    fT = nc.dram_tensor("fT", [H, 2, 128, TK], F32, kind="ExternalInput").ap()
    vtok = nc.dram_tensor("vtok", [H, 2, 128, NBLK, 128], F32, kind="ExternalInput").ap()
    lbraw = nc.dram_tensor("lbraw", [128, H, 2], F32, kind="ExternalInput").ap()
    nwd = nc.dram_tensor("nw", [128, 1], F32, kind="ExternalInput").ap()
    gT = nc.dram_tensor("gT", [H, 128, T], F32, kind="ExternalInput").ap()
    identd = nc.dram_tensor("ident", [128, 128], F32, kind="ExternalInput").ap()
    maskd = nc.dram_tensor("mask", [2, 128, 128], F32, kind="ExternalInput").ap()
    yT = nc.dram_tensor("yT", [H, 128, T], F32, kind="ExternalOutput").ap()
    with ExitStack() as st:
        c = Ctx(nc, st)
        s = Sched(nc, st)
        ident = c.sb([128, 128]); b_id = Buf()
        s.dma("sp", ident[:], identd, writes=[b_id])
        mask = c.sb([128, 2, 128]); b_mask = Buf()
        for d in range(2):
            s.dma("sp", mask[:, d, :], maskd[d], writes=[b_mask])
        lbr = c.sb([128, H, 2]); b_lb = Buf()
        s.dma("sp", lbr[:], lbraw, writes=[b_lb])
        lb = c.sb([128, H]); oml = c.sb([128, H])
        s.tt(lb[:], lbr[:, :, 1], lbr[:, :, 0], ALU.subtract, reads=[b_lb], writes=[b_lb])
        s.act(lb[:], lb[:], AF.Sigmoid, reads=[b_lb], writes=[b_lb])
        s.ts(oml[:], lb[:], -1.0, ALU.mult, 1.0, ALU.add, reads=[b_lb], writes=[b_lb])
        nw = c.sb([128, 1]); b_nw = Buf()
        s.dma("sp", nw[:], nwd, writes=[b_nw])
        onesf = c.sb([128, 128]); b_ones = Buf()
        s.op("dve", "memset", writes=[b_ones], ap=onesf[:], constant=1.0)
        epsb = c.sb([128, 1]); b_eps = Buf()
        s.op("dve", "memset", writes=[b_eps], ap=epsb[:], constant=1e-6)
        rmask = c.sb([128, TK]); b_rm = Buf()
        s.op("dve", "memset", writes=[b_rm], ap=rmask[:], constant=1.0)
        s.op("dve", "memset", reads=[b_rm], writes=[b_rm], ap=rmask[:, :].rearrange("p (c t) -> p c t", t=CH)[:, :, 0:1], constant=0.0)

        q_s = c.sb([128, TK]); f_s = c.sb([128, TK]); lg = c.sb([128, TK]); bt = c.sb([128, TK])
        b_q, b_f, b_lg, b_bt = Buf(), Buf(), Buf(), Buf()
        v_s = c.sb([128, NBLK, 128]); b_v = Buf()
        totc = c.sb([128, NCH]); dec = c.sb([128, NCH]); b_tot = Buf(); b_dec = Buf()
        osum = c.sb([128, T]); b_os = Buf()
        sg = c.sb([128, T]); b_sg = Buf()
        S = [c.sb([128, 128], name="S%d" % i) for i in range(2)]; b_S = [Buf(), Buf()]
        ktok = [c.sb([128, 128], name="ktok%d" % i) for i in range(2)]; b_ktok = [Buf(), Buf()]
        attn = [c.sb([128, 128], name="attn%d" % i) for i in range(2)]; b_attn = [Buf(), Buf()]
        pt = [c.ps([128, 512], name="pt%d" % i) for i in range(2)]; b_pt = [Buf(), Buf()]
        pa = [c.ps([128, 512], name="pa%d" % i) for i in range(2)]; b_pa = [Buf(), Buf()]
        po = [c.ps([128, 512], name="po%d" % i) for i in range(2)]; b_po = [Buf(), Buf()]
        pm = [c.ps([128, 512], name="pm%d" % i) for i in range(2)]; b_pm = [Buf(), Buf()]
        sq = c.sb([128, 512]); b_sq = Buf()
        rstd = c.sb([128, 512]); b_rstd = Buf()
        yo = [c.sb([128, 512], name="hyo%d" % i) for i in range(2)]; b_yo = [Buf(), Buf()]
        n_t = n_a = n_o = n_m = n_y = 0

        def v3(t):
            return t[:, :].rearrange("p (c t) -> p c t", t=CH)

        for h in range(H):
            s.dma("sp", sg[:], gT[h], writes=[b_sg])
            s.act(sg[:], sg[:], AF.Silu, reads=[b_sg], writes=[b_sg])
            for d in range(2):
                s.dma("sp", q_s[:], qT[h, d], writes=[b_q])
                s.dma("sp", f_s[:], fT[h, d], writes=[b_f])
                s.dma("sp", v_s[:], vtok[h, d], writes=[b_v])
                s.act(f_s[:], f_s[:], AF.Sigmoid, reads=[b_f], writes=[b_f])
                s.ts(f_s[:], f_s[:], oml[:, h:h + 1], ALU.mult, lb[:, h:h + 1], ALU.add, reads=[b_f, b_lb], writes=[b_f])
                s.act(lg[:], f_s[:], AF.Ln, reads=[b_f], writes=[b_lg])
                s.ts(f_s[:], f_s[:], -1.0, ALU.mult, 1.0, ALU.add, reads=[b_f], writes=[b_f])
                s.op("dve", "tensor_tensor_scan", reads=[b_rm, b_lg], writes=[b_bt], out=bt[:], data0=rmask[:], data1=lg[:],
                     initial=0.0, op0=ALU.mult, op1=ALU.add)
                s.cp(totc[:], v3(bt)[:, :, CH - 1], reads=[b_bt], writes=[b_tot])
                if d == 1:
                    s.tt(bt[:], lg[:], bt[:], ALU.subtract, reads=[b_lg, b_bt], writes=[b_bt])
                    s.tt(v3(bt), v3(bt), totc[:, :].unsqueeze(2).broadcast_to([128, NCH, CH]), ALU.add, reads=[b_bt, b_tot], writes=[b_bt])
                s.act(dec[:], totc[:], AF.Exp, reads=[b_tot], writes=[b_dec])
                s.act(lg[:], bt[:], AF.Exp, reads=[b_bt], writes=[b_lg])
                s.tt(q_s[:], q_s[:], lg[:], ALU.mult, reads=[b_q, b_lg], writes=[b_q])
                s.act(bt[:], bt[:], AF.Exp, reads=[b_bt], writes=[b_bt], scale=-1.0)
                s.tt(f_s[:], f_s[:], bt[:], ALU.mult, reads=[b_f, b_bt], writes=[b_f])
                s.tt(v3(bt), v3(f_s), dec[:, :].unsqueeze(2).broadcast_to([128, NCH, CH]), ALU.mult, reads=[b_f, b_dec], writes=[b_bt])
                cur = 0
                s.op("dve", "memset", writes=[b_S[0]], ap=S[0][:], constant=0.0)
                blocks = range(NBLK) if d == 0 else range(NBLK - 1, -1, -1)
                cs = (0, 1) if d == 0 else (1, 0)
                for blk in blocks:
                    bsl = slice(blk * 128, (blk + 1) * 128)
                    is_lat = (blk >= 2) if d == 0 else (blk < 16)
                    lat0 = (blk - 2) * 128 if d == 0 else blk * 128
                    ptt = pt[n_t % 2]; bptt = b_pt[n_t % 2]; kt = ktok[n_t % 2]; bkt = b_ktok[n_t % 2]; n_t += 1
                    s.op("pe", "transpose", reads=[b_bt, b_id], writes=[bptt], out=ptt[:, :128], in_=bt[:, bsl], identity=ident[:])
                    s.cp(kt[:], ptt[:, :128], reads=[bptt], writes=[bkt], eng="act")
                    if is_lat:
                        paa = pa[n_a % 2]; bpaa = b_pa[n_a % 2]; at = attn[n_a % 2]; bat = b_attn[n_a % 2]; n_a += 1
                        s.mm(paa[:, :128], f_s[:, bsl], q_s[:, bsl], reads=[b_f, b_q], writes=[bpaa])
                        s.tt(at[:], paa[:, :128], mask[:, d, :], ALU.mult, reads=[bpaa, b_mask], writes=[bat])
                        poo = po[n_o % 2]; bpoo = b_po[n_o % 2]; n_o += 1
                        s.mm(poo[:, :128], v_s[:, blk, :], at[:], start=True, stop=False, reads=[b_v, bat], writes=[bpoo])
                    for ci, cc in enumerate(cs):
                        csl = slice(blk * 128 + cc * CH, blk * 128 + (cc + 1) * CH)
                        if is_lat:
                            s.mm(poo[:, cc * CH:(cc + 1) * CH], S[cur][:], q_s[:, csl], start=False, stop=(ci == 1),
                                 reads=[b_S[cur], b_q], writes=[bpoo])
                        pmm = pm[n_m % 2]; bpmm = b_pm[n_m % 2]; n_m += 1
                        s.mm(pmm[:, :128], kt[cc * CH:(cc + 1) * CH, :], v_s[cc * CH:(cc + 1) * CH, blk, :], reads=[bkt, b_v], writes=[bpmm])
                        ch = blk * 2 + cc
                        s.stt(S[1 - cur][:], S[cur][:], dec[:, ch:ch + 1], pmm[:, :128], ALU.mult, ALU.add,
                              reads=[b_S[cur], b_dec, bpmm], writes=[b_S[1 - cur]])
                        cur = 1 - cur
                    if is_lat:
                        if d == 0:
                            s.cp(osum[:, lat0:lat0 + 128], poo[:, :128], reads=[bpoo], writes=[b_os], eng="act")
                        else:
                            s.tt(osum[:, lat0:lat0 + 128], poo[:, :128], osum[:, lat0:lat0 + 128], ALU.add, reads=[bpoo, b_os], writes=[b_os])
                if cur == 1:
                    pass
            for t in range(4):
                tsl = slice(t * 512, (t + 1) * 512)
                paa = pa[n_a % 2]; bpaa = b_pa[n_a % 2]; n_a += 1
                s.act(sq[:], osum[:, tsl], AF.Square, reads=[b_os], writes=[b_sq])
                s.mm(paa[:, :512], onesf[:], sq[:], reads=[b_ones, b_sq], writes=[bpaa])
                s.act(rstd[:], paa[:, :512], AF.Sqrt, reads=[bpaa, b_eps], writes=[b_rstd], bias=epsb[:], scale=1.0 / 128)
                s.op("dve", "reciprocal", reads=[b_rstd], writes=[b_rstd], out=rstd[:], in_=rstd[:])
                y = yo[n_y % 2]; by = b_yo[n_y % 2]; n_y += 1
                s.tt(y[:], osum[:, tsl], rstd[:], ALU.mult, reads=[b_os, b_rstd], writes=[by])
                s.stt(y[:], y[:], nw[:, 0:1], sg[:, tsl], ALU.mult, ALU.mult, reads=[by, b_nw, b_sg], writes=[by])
                s.dma("sp", yT[h, :, tsl], y[:], reads=[by])
        s.finish()
        s.emit()
    return nc


CH = 64
TK = 2304
NCH = 36
LWS = -0.6065306597126334


def rw_masks():
    i = np.arange(64)[:, None]; t = np.arange(64)[None, :]
    m2 = np.stack([np.concatenate([i < t, i <= t], 1), np.concatenate([i > t, i >= t], 1)]).astype(np.float32)
    mT = np.stack([(t < i), (t > i)]).astype(np.float32)
    return m2, mT


def build_rw(NH=8):
    nc = bass.Bass("TRN2", target_bir_lowering=False)
    zr = nc.dram_tensor("zr", [NH, 3, 64, TK], F32, kind="ExternalInput").ap()
    mu = nc.dram_tensor("mu", [64, NH, 3, 2], F32, kind="ExternalInput").ap()
    zwa = nc.dram_tensor("zwa", [4, 64, TK], F32, kind="ExternalInput").ap()
    muwa = nc.dram_tensor("muwa", [64, 4, 2], F32, kind="ExternalInput").ap()
    w2d = nc.dram_tensor("w2", [64, 2, NH, 64], F32, kind="ExternalInput").ap()
    a2d = nc.dram_tensor("a2", [64, 2, NH, 64], F32, kind="ExternalInput").ap()
    w0d = nc.dram_tensor("w0", [64, 2, NH], F32, kind="ExternalInput").ap()
    a0d = nc.dram_tensor("a0", [64, 2, NH], F32, kind="ExternalInput").ap()
    kvec = nc.dram_tensor("kvec", [64, 3, NH], F32, kind="ExternalInput").ap()
    lnd = nc.dram_tensor("ln", [64, 2, NH, 64], F32, kind="ExternalInput").ap()
    gz = nc.dram_tensor("gz", [NH, 64, NCH, 64], F32, kind="ExternalInput").ap()
    m2d = nc.dram_tensor("m2", [2, 64, 128], F32, kind="ExternalInput").ap()
    mTd = nc.dram_tensor("mT", [2, 64, 64], F32, kind="ExternalInput").ap()
    identd = nc.dram_tensor("ident", [64, 64], F32, kind="ExternalInput").ap()
    yout = nc.dram_tensor("y", [NH, 64, NCH, 64], F32, kind="ExternalOutput").ap()
    P = 64
    with ExitStack() as st:
        c = Ctx(nc, st)
        s = Sched(nc, st)

        def T_(shape, name):
            return c.sb(shape, F32, name="s_" + name), Buf(name)

        ident, b_id = T_([P, 64], "ident")
        s.dma("sp", ident[:], identd, writes=[b_id])
        m2, b_m2 = T_([P, 2, 128], "m2"); mT, b_mT = T_([P, 2, 64], "mT")
        for d in range(2):
            s.dma("sp", m2[:, d, :], m2d[d], writes=[b_m2])
            s.dma("sp", mT[:, d, :], mTd[d], writes=[b_mT])
        mus, b_mu = T_([P, NH, 3, 2], "mus"); s.dma("sp", mus[:], mu, writes=[b_mu])
        muw, b_muw = T_([P, 4, 2], "muw"); s.dma("sp", muw[:], muwa, writes=[b_muw])
        w2, b_w2 = T_([P, 2, NH, 64], "w2"); s.dma("sp", w2[:], w2d, writes=[b_w2])
        a2, b_a2 = T_([P, 2, NH, 64], "a2"); s.dma("sp", a2[:], a2d, writes=[b_a2])
        w0, b_w0 = T_([P, 2, NH], "w0"); s.dma("sp", w0[:], w0d, writes=[b_w0])
        a0, b_a0 = T_([P, 2, NH], "a0"); s.dma("sp", a0[:], a0d, writes=[b_a0])
        kv, b_kv = T_([P, 3, NH], "kv"); s.dma("sp", kv[:], kvec, writes=[b_kv])
        omka, b_omka = T_([P, NH], "omka")
        s.ts(omka[:], kv[:, 1, :], -1.0, ALU.mult, 1.0, ALU.add, reads=[b_kv], writes=[b_omka])
        ln, b_ln = T_([P, 2, NH, 64], "ln"); s.dma("sp", ln[:], lnd, writes=[b_ln])
        ones, b_ones = T_([P, 64], "ones")
        s.op("dve", "memset", writes=[b_ones], ap=ones[:], constant=1.0)
        rmask, b_rm = T_([P, TK], "rmask")
        s.op("dve", "memset", writes=[b_rm], ap=rmask[:], constant=1.0)
        s.op("dve", "memset", reads=[b_rm], writes=[b_rm], ap=rmask[:, :].rearrange("p (c t) -> p c t", t=CH)[:, :, 0:1], constant=0.0)
        c0, b_c0 = T_([P, NH, 3], "c0")
        s.tt(c0[:], mus[:, :, :, 0], mus[:, :, :, 1], ALU.add, reads=[b_mu], writes=[b_c0])
        s.ts(c0[:], c0[:], -1.0, ALU.mult, 1.0, ALU.add, reads=[b_c0], writes=[b_c0])
        c0w, b_c0w = T_([P, 4], "c0w")
        s.tt(c0w[:], muw[:, :, 0], muw[:, :, 1], ALU.add, reads=[b_muw], writes=[b_c0w])
        s.ts(c0w[:], c0w[:], -1.0, ALU.mult, 1.0, ALU.add, reads=[b_c0w], writes=[b_c0w])

        def v3(t, w=CH):
            return t[:, :].rearrange("p (c t) -> p c t", t=w)

        zt, b_zt = T_([P, TK], "zt")

        def shift(dst, bdst, src_dram, c0ap, mp, mn, cbufs):
            s.dma("sp", zt[:], src_dram, writes=[b_zt])
            s.ts(dst[:], zt[:], c0ap, ALU.mult, reads=[b_zt] + cbufs, writes=[bdst])
            for (lo, hi) in ((1, 256), (257, TK)):
                s.stt(dst[:, lo:hi], zt[:, lo - 1:hi - 1], mp, dst[:, lo:hi], ALU.mult, ALU.add, reads=[b_zt, bdst] + cbufs, writes=[bdst])
            for (lo, hi) in ((0, 255), (256, TK - 1)):
                s.stt(dst[:, lo:hi], zt[:, lo + 1:hi + 1], mn, dst[:, lo:hi], ALU.mult, ALU.add, reads=[b_zt, bdst] + cbufs, writes=[bdst])

        wa = []
        for j in range(4):
            t, bt_ = T_([P, TK], "wa%d" % j)
            shift(t, bt_, zwa[j], c0w[:, j:j + 1], muw[:, j, 0:1], muw[:, j, 1:2], [b_c0w, b_muw])
            if j < 2:
                s.act(t[:], t[:], AF.Tanh, reads=[bt_], writes=[bt_])
            wa.append((t, bt_))

        r_, b_r = T_([P, TK], "r"); k_, b_k = T_([P, TK], "k"); kk, b_kk = T_([P, TK], "kk")
        vtok, b_vtok = T_([P, NCH, 64], "vtok"); ytok, b_ytok = T_([P, NCH, 64], "ytok"); rks, b_rks = T_([P, TK], "rks")
        A1, b_A1 = T_([P, TK], "A1"); A2, b_A2 = T_([P, TK], "A2"); A3, b_A3 = T_([P, TK], "A3"); A4, b_A4 = T_([P, TK], "A4")
        AR, b_AR = T_([P, NCH, 128], "AR")
        tot, b_tot = T_([P, NCH], "tot"); gam, b_gam = T_([P, NCH], "gam")
        S = [T_([P, 64], "S%d" % i) for i in range(2)]
        psr = [c.ps([128, 512], name="psr%d" % i) for i in range(8)]
        b_psr = [Buf() for _ in range(8)]
        state = {"p": 0, "e": 0}

        def PS():
            i = state["p"] % 8; state["p"] += 1
            return psr[i], b_psr[i]

        def EV():
            state["e"] += 1
            return "act" if state["e"] % 2 else "dve"

        def ring(name, shape, n):
            return [T_(shape, "%s%d" % (name, i)) for i in range(n)]
        Ab_r = ring("Ab", [P, 128], 2); Ak_r = ring("Ak", [P, 128], 2)
        Pm_r = ring("Pm", [P, 64], 3); Qm_r = ring("Qm", [P, 64], 3); T_r = ring("Tm", [P, 64], 3)
        bh_r = ring("bh", [P, 64], 2); kh_r = ring("kh", [P, 64], 2)
        X_r = ring("X", [P, 64], 2); U_r = ring("U", [P, 64], 2)
        cnt = {"Ab": 0, "Ak": 0, "P": 0, "Q": 0, "T": 0, "bh": 0, "kh": 0, "X": 0, "U": 0}

        def nxt(r, key):
            i = cnt[key] % len(r); cnt[key] += 1
            return r[i]

        bon, b_bon = T_([P, NCH], "bon")
        mean, b_mean = T_([P, NCH], "mean"); var, b_var = T_([P, NCH], "var")
        epsb, b_eps = T_([P, 1], "eps")
        s.op("dve", "memset", writes=[b_eps], ap=epsb[:], constant=64e-5)

        for h in range(NH):
            shift(r_, b_r, zr[h, 0], c0[:, h, 0:1], mus[:, h, 0, 0:1], mus[:, h, 0, 1:2], [b_c0, b_mu])
            shift(k_, b_k, zr[h, 1], c0[:, h, 1:2], mus[:, h, 1, 0:1], mus[:, h, 1, 1:2], [b_c0, b_mu])
            shift(A1, b_A1, zr[h, 2], c0[:, h, 2:3], mus[:, h, 2, 0:1], mus[:, h, 2, 1:2], [b_c0, b_mu])
            for ch in range(NCH):
                ps, bps = PS()
                s.mm(ps[0:64, 0:64], A1[:, ch * CH:(ch + 1) * CH], ident[:], reads=[b_A1, b_id], writes=[bps])
                s.cp(vtok[:, ch, :], ps[0:64, 0:64], reads=[bps], writes=[b_vtok], eng=EV())
            s.ts(kk[:], k_[:], kv[:, 0, h:h + 1], ALU.mult, reads=[b_k, b_kv], writes=[b_kk])
            s.tt(A2[:], kk[:], kk[:], ALU.mult, reads=[b_kk], writes=[b_A2])
            for t0 in range(0, TK, 512):
                w = min(512, TK - t0)
                ps, bps = PS()
                s.mm(ps[0:64, :w], ones[:], A2[:, t0:t0 + w], reads=[b_ones, b_A2], writes=[bps])
                s.act(A3[:, t0:t0 + w], ps[0:64, :w], AF.Sqrt, reads=[bps], writes=[b_A3])
            s.ts(A3[:], A3[:], 1e-12, ALU.max, reads=[b_A3], writes=[b_A3])
            s.op("dve", "reciprocal", reads=[b_A3], writes=[b_A3], out=A3[:], in_=A3[:])
            s.tt(kk[:], kk[:], A3[:], ALU.mult, reads=[b_kk, b_A3], writes=[b_kk])
            for d in range(2):
                wd_t, b_wd = wa[d]; ad_t, b_ad = wa[2 + d]
                for t0 in range(0, TK, 512):
                    w = min(512, TK - t0)
                    ps, bps = PS()
                    s.mm(ps[0:64, :w], w2[:, d, h, :], wd_t[:, t0:t0 + w], reads=[b_w2, b_wd], writes=[bps])
                    s.act(A1[:, t0:t0 + w], ps[0:64, :w], AF.Sigmoid, reads=[bps, b_w0], writes=[b_A1], bias=w0[:, d, h:h + 1], scale=1.0)
                    ps, bps = PS()
                    s.mm(ps[0:64, :w], a2[:, d, h, :], ad_t[:, t0:t0 + w], reads=[b_a2, b_ad], writes=[bps])
                    s.act(A2[:, t0:t0 + w], ps[0:64, :w], AF.Sigmoid, reads=[bps, b_a0], writes=[b_A2], bias=a0[:, d, h:h + 1], scale=1.0)
                s.ts(A1[:], A1[:], LWS, ALU.mult, reads=[b_A1], writes=[b_A1])
                s.op("dve", "tensor_tensor_scan", reads=[b_rm, b_A1], writes=[b_A3], out=A3[:], data0=rmask[:], data1=A1[:],
                     initial=0.0, op0=ALU.mult, op1=ALU.add)
                s.cp(tot[:], v3(A3)[:, :, CH - 1], reads=[b_A3], writes=[b_tot])
                if d == 1:
                    s.tt(A3[:], A1[:], A3[:], ALU.subtract, reads=[b_A1, b_A3], writes=[b_A3])
                    s.tt(v3(A3), v3(A3), tot[:, :].unsqueeze(2).broadcast_to([P, NCH, CH]), ALU.add, reads=[b_A3, b_tot], writes=[b_A3])
                s.act(gam[:], tot[:], AF.Exp, reads=[b_tot], writes=[b_gam])
                s.act(A4[:], A3[:], AF.Exp, reads=[b_A3], writes=[b_A4])
                s.act(A1[:], A1[:], AF.Exp, reads=[b_A1], writes=[b_A1], scale=-1.0)
                s.tt(AR[:, :, 0:64], v3(kk), v3(A4), ALU.mult, reads=[b_kk, b_A4], writes=[b_AR])
                s.stt(AR[:, :, 0:64], AR[:, :, 0:64], -1.0, v3(A1), ALU.mult, ALU.mult, reads=[b_AR, b_A1], writes=[b_AR])
                s.tt(AR[:, :, 64:128], v3(r_), v3(A4), ALU.mult, reads=[b_r, b_A4], writes=[b_AR])
                s.act(A4[:], A3[:], AF.Exp, reads=[b_A3], writes=[b_A4], scale=-1.0)
                s.ts(A3[:], A2[:], kv[:, 1, h:h + 1], ALU.mult, omka[:, h:h + 1], ALU.add, reads=[b_A2, b_kv, b_omka], writes=[b_A3])
                s.tt(A3[:], A3[:], k_[:], ALU.mult, reads=[b_A3, b_k], writes=[b_A3])
                s.tt(A1[:], A3[:], r_[:], ALU.mult, reads=[b_A3, b_r], writes=[b_A1])
                if d == 0:
                    s.ts(rks[:], A1[:], kv[:, 2, h:h + 1], ALU.mult, reads=[b_A1, b_kv], writes=[b_rks])
                else:
                    s.stt(rks[:], A1[:], kv[:, 2, h:h + 1], rks[:], ALU.mult, ALU.add, reads=[b_A1, b_kv, b_rks], writes=[b_rks])
                s.tt(A3[:], A3[:], A4[:], ALU.mult, reads=[b_A3, b_A4], writes=[b_A3])
                s.tt(A2[:], A2[:], kk[:], ALU.mult, reads=[b_A2, b_kk], writes=[b_A2])
                s.tt(A2[:], A2[:], A4[:], ALU.mult, reads=[b_A2, b_A4], writes=[b_A2])
                gb = gam[:, :].unsqueeze(2).broadcast_to([P, NCH, CH])
                s.tt(v3(A1), v3(A2), gb, ALU.mult, reads=[b_A2, b_gam], writes=[b_A1])
                s.tt(v3(A4), v3(A3), gb, ALU.mult, reads=[b_A3, b_gam], writes=[b_A4])
                cur = 0
                s.op("dve", "memset", writes=[S[0][1]], ap=S[0][0][:], constant=0.0)
                order = list(range(NCH)) if d == 0 else [3, 2, 1, 0] + list(range(NCH - 1, 3, -1))
                for ch in order:
                    csl = slice(ch * CH, (ch + 1) * CH)
                    Ab, b_Ab = nxt(Ab_r, "Ab"); Ak, b_Ak = nxt(Ak_r, "Ak")
                    ps, bps = PS()
                    s.mm(ps[0:64, 0:128], A2[:, csl], AR[:, ch, :], reads=[b_A2, b_AR], writes=[bps])
                    s.tt(Ab[:], ps[0:64, 0:128], m2[:, d, :], ALU.mult, reads=[bps, b_m2], writes=[b_Ab])
                    ps, bps = PS()
                    s.mm(ps[0:64, 0:128], A3[:, csl], AR[:, ch, :], reads=[b_A3, b_AR], writes=[bps])
                    s.tt(Ak[:], ps[0:64, 0:128], m2[:, d, :], ALU.mult, reads=[bps, b_m2], writes=[b_Ak])
                    Q, b_Q = nxt(Qm_r, "Q")
                    ps, bps = PS()
                    s.mm(ps[0:64, 0:64], AR[:, ch, 0:64], A2[:, csl], reads=[b_A2, b_AR], writes=[bps])
                    s.tt(Q[:], ps[0:64, 0:64], mT[:, d, :], ALU.mult, reads=[bps, b_mT], writes=[b_Q])
                    Pm, b_P = Ab[:, 0:64], b_Ab
                    Tm, b_T = nxt(T_r, "T")
                    s.tt(Tm[:], Ab[:, 0:64], ident[:], ALU.add, reads=[b_Ab, b_id], writes=[b_T])
                    for m in range(1, 6):
                        ps, bps = PS()
                        s.mm(ps[0:64, 0:64], Pm, Q[:], reads=[b_P, b_Q], writes=[bps])
                        Qn, b_Qn = nxt(Qm_r, "Q")
                        s.cp(Qn[:], ps[0:64, 0:64], reads=[bps], writes=[b_Qn], eng=EV())
                        if m < 5:
                            ps, bps = PS()
                            s.mm(ps[0:64, 0:64], Q[:], Pm, reads=[b_P, b_Q], writes=[bps])
                            Pn, b_Pn = nxt(Pm_r, "P")
                            s.cp(Pn[:], ps[0:64, 0:64], reads=[bps], writes=[b_Pn], eng=EV())
                        ps, bps = PS()
                        s.mm(ps[0:64, 0:64], Qn[:], Tm[:], reads=[b_Qn, b_T], writes=[bps])
                        Tn, b_Tn = nxt(T_r, "T")
                        s.tt(Tn[:], ps[0:64, 0:64], Tm[:], ALU.add, reads=[bps, b_T], writes=[b_Tn])
                        Tm, b_T = Tn, b_Tn
                        Q, b_Q = Qn, b_Qn
                        if m < 5:
                            Pm, b_P = Pn[:], b_Pn
                    bh, b_bh = nxt(bh_r, "bh"); kh, b_kh = nxt(kh_r, "kh")
                    ps, bps = PS()
                    s.mm(ps[0:64, 0:64], A1[:, csl], ident[:], reads=[b_A1, b_id], writes=[bps])
                    s.cp(bh[:], ps[0:64, 0:64], reads=[bps], writes=[b_bh], eng=EV())
                    ps, bps = PS()
                    s.mm(ps[0:64, 0:64], A4[:, csl], ident[:], reads=[b_A4, b_id], writes=[bps])
                    s.cp(kh[:], ps[0:64, 0:64], reads=[bps], writes=[b_kh], eng=EV())
                    Sc, b_Sc = S[cur]; Sn, b_Sn = S[1 - cur]
                    X, b_X = nxt(X_r, "X"); U, b_U = nxt(U_r, "U")
                    ps, bps = PS()
                    s.mm(ps[0:64, 0:64], AR[:, ch, 0:64], Sc[:], start=True, stop=False, reads=[b_AR, b_Sc], writes=[bps])
                    s.mm(ps[0:64, 0:64], Ak[:, 0:64], vtok[:, ch, :], start=False, stop=True, reads=[b_Ak, b_vtok], writes=[bps])
                    s.cp(X[:], ps[0:64, 0:64], reads=[bps], writes=[b_X], eng="act")
                    ps, bps = PS()
                    s.mm(ps[0:64, 0:64], Tm[:], X[:], reads=[b_T, b_X], writes=[bps])
                    s.cp(U[:], ps[0:64, 0:64], reads=[bps], writes=[b_U], eng="act")
                    psy, bpsy = PS()
                    s.mm(psy[0:64, 0:64], AR[:, ch, 64:128], Sc[:], start=True, stop=False, reads=[b_AR, b_Sc], writes=[bpsy])
                    s.mm(psy[0:64, 0:64], Ab[:, 64:128], U[:], start=False, stop=False, reads=[b_Ab, b_U], writes=[bpsy])
                    s.mm(psy[0:64, 0:64], Ak[:, 64:128], vtok[:, ch, :], start=False, stop=True, reads=[b_Ak, b_vtok], writes=[bpsy])
                    if d == 0:
                        s.cp(ytok[:, ch, :], psy[0:64, 0:64], reads=[bpsy], writes=[b_ytok], eng="act")
                    else:
                        s.tt(ytok[:, ch, :], psy[0:64, 0:64], ytok[:, ch, :], ALU.add, reads=[bpsy, b_ytok], writes=[b_ytok])
                    ps, bps = PS()
                    s.mm(ps[0:64, 0:64], bh[:], U[:], start=True, stop=False, reads=[b_bh, b_U], writes=[bps])
                    s.mm(ps[0:64, 0:64], kh[:], vtok[:, ch, :], start=False, stop=True, reads=[b_kh, b_vtok], writes=[bps])
                    s.stt(Sn[:], Sc[:], gam[:, ch:ch + 1], ps[0:64, 0:64], ALU.mult, ALU.add, reads=[b_Sc, b_gam, bps], writes=[b_Sn])
                    cur = 1 - cur
            for ch in range(NCH):
                ps, bps = PS()
                s.mm(ps[0:64, 0:1], rks[:, ch * CH:(ch + 1) * CH], ones[:, 0:1], reads=[b_rks, b_ones], writes=[bps])
                s.cp(bon[:, ch:ch + 1], ps[0:64, 0:1], reads=[bps], writes=[b_bon], eng=EV())
            s.op("dve", "tensor_reduce", reads=[b_ytok], writes=[b_mean], out=mean[:], in_=ytok[:], axis=AX.X, op=ALU.add)
            s.ts(mean[:], mean[:], 1.0 / 64, ALU.mult, reads=[b_mean], writes=[b_mean])
            s.tt(ytok[:], ytok[:], mean[:, :].unsqueeze(2).broadcast_to([P, NCH, 64]), ALU.subtract, reads=[b_ytok, b_mean], writes=[b_ytok])
            A1v = A1[:, :].rearrange("p (c t) -> p c t", t=64)
            s.tt(A1v, ytok[:], ytok[:], ALU.mult, reads=[b_ytok], writes=[b_A1])
            s.op("dve", "tensor_reduce", reads=[b_A1], writes=[b_var], out=var[:], in_=A1v, axis=AX.X, op=ALU.add)
            s.act(var[:], var[:], AF.Sqrt, reads=[b_var, b_eps], writes=[b_var], bias=epsb[:], scale=1.0 / 64)
            s.op("dve", "reciprocal", reads=[b_var], writes=[b_var], out=var[:], in_=var[:])
            s.tt(ytok[:], ytok[:], var[:, :].unsqueeze(2).broadcast_to([P, NCH, 64]), ALU.mult, reads=[b_ytok, b_var], writes=[b_ytok])
            lnw = ln[:, 0, h, :].unsqueeze(1).broadcast_to([P, NCH, 64]); lnb = ln[:, 1, h, :].unsqueeze(1).broadcast_to([P, NCH, 64])
            s.tt(ytok[:], ytok[:], lnw, ALU.mult, reads=[b_ytok, b_ln], writes=[b_ytok])
            s.tt(ytok[:], ytok[:], lnb, ALU.add, reads=[b_ytok, b_ln], writes=[b_ytok])
            s.tt(A1v, vtok[:], bon[:, :].unsqueeze(2).broadcast_to([P, NCH, 64]), ALU.mult, reads=[b_vtok, b_bon], writes=[b_A1])
            s.tt(ytok[:], ytok[:], A1v, ALU.add, reads=[b_ytok, b_A1], writes=[b_ytok])
            A2v = A2[:, :].rearrange("p (c t) -> p c t", t=64)
            s.dma("sp", A2v, gz[h], writes=[b_A2])
            s.act(A2[:], A2[:], AF.Silu, reads=[b_A2], writes=[b_A2])
            s.tt(ytok[:], ytok[:], A2v, ALU.mult, reads=[b_ytok, b_A2], writes=[b_ytok])
            s.dma("sp", yout[h], ytok[:], reads=[b_ytok])
        s.finish()
        s.emit()
    return nc

EVEN_IN = 8448
ODD_IN = 7232
ODD_PAD = 7296
S2 = 6400
O3 = 1088
O4 = 5184


def _blk(w):
    K, N = w.shape
    return np.ascontiguousarray(w.reshape(16, 128, N // 128, 128).transpose(2, 1, 0, 3))


def _fm(v):
    return np.ascontiguousarray(v.reshape(16, 128).T)


def _swap_pairs(a, axis):
    idx = np.arange(a.shape[axis]) ^ 1
    return np.take(a, idx, axis=axis)


def _chunkT(a):
    return np.ascontiguousarray(a.T.reshape(4, 128, -1).transpose(1, 0, 2))


def _wblk(w):
    return np.ascontiguousarray(w.reshape(4, 128, -1).transpose(1, 0, 2))


def _rope_tables(T):
    t = np.arange(T)
    pos = np.stack([t // 64, t % 64], -1).astype(np.float32)
    inv = (np.float32(10000.0) ** (-np.arange(16, dtype=np.float32) / np.float32(16))).astype(np.float32)
    ang = (pos[:, :, None] * inv).reshape(T, 32)
    cos = np.repeat(np.cos(ang), 2, axis=1).T
    sin = np.repeat(np.sin(ang), 2, axis=1).T
    sgn = np.tile(np.array([-1.0, 1.0], np.float32), 32)[:, None]
    return np.ascontiguousarray(cos.astype(np.float32)), np.ascontiguousarray((sin * sgn).astype(np.float32))


def _run(nc, in_maps):
    res = run_bass_kernel_spmd(nc, in_maps, core_ids=list(range(8)))
    return res.results


def _rows(xb, cb, hh):
    return np.concatenate([xb[hh * 1024:(hh + 1) * 1024], cb[hh * 128:(hh + 1) * 128]], 0)


def _gather_tok(res, key, ncols):
    lat = np.empty((4, 2048, ncols), np.float32); cx = np.empty((4, 256, ncols), np.float32)
    for ci in range(8):
        b, hh = ci // 2, ci % 2
        a = res[ci][key][:ncols]
        lat[b, hh * 1024:(hh + 1) * 1024] = a[:, :1024].T
        cx[b, hh * 128:(hh + 1) * 128] = a[:, 1024:].T
    return lat, cx


def _rw_host(zl, zc, gl, gc, heads_, P):
    mu, w0, w2, a0, a2, k_k, k_a, r_k, ln_w, ln_b = P
    zall = np.concatenate([zc, zl], 0)
    gall = np.concatenate([gc, gl], 0)
    nh = len(heads_)
    zr = np.stack([np.stack([zall[:, q * 1024 + h * 64: q * 1024 + (h + 1) * 64].T for q in range(3)]) for h in heads_])
    mu_ = np.stack([np.stack([np.stack([mu[0, q * 1024 + h * 64:q * 1024 + (h + 1) * 64], mu[1, q * 1024 + h * 64:q * 1024 + (h + 1) * 64]], -1)
                              for q in range(3)], 1) for h in heads_], 1)
    zwa = np.stack([zall[:, 3072 + j * 64:3072 + (j + 1) * 64].T for j in range(4)])
    muwa = np.stack([np.stack([mu[0, 3072 + j * 64:3072 + (j + 1) * 64], mu[1, 3072 + j * 64:3072 + (j + 1) * 64]], -1) for j in range(4)], 1)
    w2h = np.stack([np.stack([w2[d][:, h * 64:(h + 1) * 64] for h in heads_], 1) for d in range(2)], 1)
    a2h = np.stack([np.stack([a2[d][:, h * 64:(h + 1) * 64] for h in heads_], 1) for d in range(2)], 1)
    w0h = np.stack([np.stack([w0[d][h * 64:(h + 1) * 64] for h in heads_], 1) for d in range(2)], 1)
    a0h = np.stack([np.stack([a0[d][h * 64:(h + 1) * 64] for h in heads_], 1) for d in range(2)], 1)
    kvec = np.stack([np.stack([k_k[h * 64:(h + 1) * 64] for h in heads_], 1), np.stack([k_a[h * 64:(h + 1) * 64] for h in heads_], 1),
                     np.stack([r_k[h] for h in heads_], 1)], 1)
    ln = np.stack([np.stack([ln_w[h * 64:(h + 1) * 64] for h in heads_]), np.stack([ln_b[h * 64:(h + 1) * 64] for h in heads_])])
    ln = np.broadcast_to(ln[None], (64, 2, nh, 64))
    gz = np.stack([gall[:, h * 64:(h + 1) * 64].reshape(36, 64, 64).transpose(1, 0, 2) for h in heads_])
    m2, mT = rw_masks()
    f = lambda a: np.ascontiguousarray(a, dtype=np.float32)
    return {"zr": f(zr), "mu": f(mu_), "zwa": f(zwa), "muwa": f(muwa), "w2": f(w2h), "a2": f(a2h), "w0": f(w0h), "a0": f(a0h),
            "kvec": f(kvec), "ln": f(ln), "gz": f(gz), "m2": m2, "mT": mT, "ident": np.eye(64, dtype=np.float32)}


def kernel(x, c, ctx, c_ctx, ada_w, ada_b, norm_w, e_w_in, e_w_out, na_rpb, rw_mu, rw_w0, rw_w2,
           rw_a0, rw_a2, rw_k_k, rw_k_a, rw_r_k, rw_ln_w, rw_ln_b, o_w_in, o_w_out, mla_q_norm,
           mla_w_uq, mla_kv_norm, mla_w_ukv, hg_lower_bounds, hg_norm_w, final_norm_w):
    A = lambda a: np.asarray(a, dtype=np.float32)
    x, c, ctx, c_ctx, ada_w, ada_b, norm_w = A(x), A(c), A(ctx), A(c_ctx), A(ada_w), A(ada_b), A(norm_w)
    e_w_in, e_w_out, o_w_in, o_w_out = A(e_w_in)[0], A(e_w_out)[0], A(o_w_in)[0], A(o_w_out)[0]
    cores = [(ci // 2, ci % 2) for ci in range(8)]
    csil = [np.ascontiguousarray(np.stack([_fm(c[b]), _fm(c_ctx)], -1)) for b, hh in cores]

    ncA = build_toklocal(EVEN_IN, False)
    adawA = _blk(ada_w[0][:, :4096]); adabA = np.ascontiguousarray(ada_b[0][:4096].reshape(32, 128).T)
    winA = _blk(e_w_in); nwA = _fm(norm_w[0])
    xT0 = [np.ascontiguousarray(_rows(x[b], ctx[b], hh).T) for b, hh in cores]
    res = _run(ncA, [{"xT": xT0[ci], "csil": csil[ci], "adaw": adawA, "adab": adabA, "normw": nwA, "win": winA} for ci in range(8)])
    zl, zc = _gather_tok(res, "zT", EVEN_IN)
    del res

    ncN = build_na()
    biasall = na_bias_host(A(na_rpb)[0])
    ims = []
    for b, hh in cores:
        hs = [hh * 4 + i for i in range(4)]
        zz = np.concatenate([zl[b], zc[b]], 0)
        ims.append({
            "qT": np.ascontiguousarray(np.stack([zl[b][:, g * 128:(g + 1) * 128].T for g in hs])),
            "kT": np.ascontiguousarray(np.stack([zz[:, 1024 + g * 128:1024 + (g + 1) * 128].T for g in hs])),
            "v": np.ascontiguousarray(np.stack([zz[:, 2048 + g * 128:2048 + (g + 1) * 128].reshape(18, 128, 128).transpose(1, 0, 2) for g in hs])),
            "gT": np.ascontiguousarray(np.stack([zl[b][:, S2 + g * 128:S2 + (g + 1) * 128].T for g in hs])),
            "bias": np.ascontiguousarray(biasall[hs]),
            "qcT": np.ascontiguousarray(np.stack([zc[b][:, g * 128:(g + 1) * 128].T for g in hs])),
            "gcT": np.ascontiguousarray(np.stack([zc[b][:, S2 + g * 128:S2 + (g + 1) * 128].T for g in hs]))})
    res = _run(ncN, ims)
    y0 = np.empty((4, 2048, 2048), np.float32); yc0 = np.empty((4, 256, 2048), np.float32)
    for ci, (b, hh) in enumerate(cores):
        for i in range(4):
            g = hh * 4 + i
            y0[b][:, g * 128:(g + 1) * 128] = res[ci]["yT"][i].T
            yc0[b][:, g * 128:(g + 1) * 128] = res[ci]["ycT"][i].T
    del res, ims

    ncR = build_rw(8)
    P = (A(rw_mu)[0], A(rw_w0)[0], A(rw_w2)[0], A(rw_a0)[0], A(rw_a2)[0], A(rw_k_k)[0], A(rw_k_a)[0], A(rw_r_k)[0], A(rw_ln_w)[0], A(rw_ln_b)[0])
    ims = []
    for b, hh in cores:
        hs = [hh * 8 + i for i in range(8)]
        ims.append(_rw_host(zl[b][:, 3072:S2], zc[b][:, 3072:S2], zl[b][:, S2 + 1024:S2 + 2048], zc[b][:, S2 + 1024:S2 + 2048], hs, P))
    res = _run(ncR, ims)
    for ci, (b, hh) in enumerate(cores):
        yy = res[ci]["y"].transpose(0, 2, 1, 3).reshape(8, 2304, 64)
        for i in range(8):
            g = hh * 8 + i
            yc0[b][:, 1024 + g * 64:1024 + (g + 1) * 64] = yy[i, :256]
            y0[b][:, 1024 + g * 64:1024 + (g + 1) * 64] = yy[i, 256:]
    del res, ims, zl, zc

    ncC = build_toklocal(ODD_PAD, True)
    adawC = _blk(np.concatenate([ada_w[0][:, 4096:], ada_w[1][:, :4096]], 1))
    adabC = np.ascontiguousarray(np.concatenate([ada_b[0][4096:], ada_b[1][:4096]]).reshape(48, 128).T)
    winC = _blk(np.concatenate([o_w_in, np.zeros((2048, ODD_PAD - ODD_IN), np.float32)], 1))
    woutC = _blk(e_w_out); nwC = _fm(norm_w[1])
    res = _run(ncC, [{"xT": xT0[ci], "csil": csil[ci], "adaw": adawC, "adab": adabC, "normw": nwC, "win": winC,
                      "yT": np.ascontiguousarray(_rows(y0[b], yc0[b], hh).T), "wout": woutC} for ci, (b, hh) in enumerate(cores)])
    xT1 = [np.ascontiguousarray(res[ci]["xnT"]) for ci in range(8)]
    zl, zc = _gather_tok(res, "zT", ODD_IN)
    del res, y0, yc0

    ncM = build_mla()
    cosT, sinT = _rope_tables(2048)
    w_uq, w_ukv = A(mla_w_uq)[0], A(mla_w_ukv)[0]
    qn_w, kvn_w = A(mla_q_norm)[0], A(mla_kv_norm)[0]
    ims = []
    for b, hh in cores:
        hs = [hh * 4 + i for i in range(4)]
        zz = np.concatenate([zl[b], zc[b]], 0)
        ims.append({"cqT": _chunkT(zl[b][:, :512]), "ckvT": _chunkT(zz[:, 512:1024]), "kpeT": np.ascontiguousarray(zz[:, 1024:1088].T),
                    "kpeS": np.ascontiguousarray(_swap_pairs(zl[b][:, 1024:1088], 1).T), "cosT": cosT, "sinT": sinT,
                    "qnorm": np.ascontiguousarray(qn_w.reshape(4, 128).T), "kvnorm": np.ascontiguousarray(kvn_w.reshape(4, 128).T),
                    "wqn": np.stack([_wblk(w_uq[:, g * 192:g * 192 + 128]) for g in hs]),
                    "wqp": np.stack([_wblk(w_uq[:, g * 192 + 128:g * 192 + 192]) for g in hs]),
                    "wqs": np.stack([_wblk(_swap_pairs(w_uq[:, g * 192 + 128:g * 192 + 192], 1)) for g in hs]),
                    "wkn": np.stack([_wblk(w_ukv[:, g * 256:g * 256 + 128]) for g in hs]),
                    "wv": _wblk(np.concatenate([w_ukv[:, g * 256 + 128:g * 256 + 256] for g in hs], 1)),
                    "gT": np.ascontiguousarray(np.stack([zl[b][:, O4 + g * 128:O4 + (g + 1) * 128].T for g in hs]))})
    res = _run(ncM, ims)
    y1 = np.zeros((4, 2048, 2048), np.float32)
    for ci, (b, hh) in enumerate(cores):
        for i in range(4):
            g = hh * 4 + i
            y1[b][:, g * 128:(g + 1) * 128] = res[ci]["yT"][i].T
    del res, ims

    ncH = build_hg()
    mF, mB = chunk_masks()
    hgl = A(hg_lower_bounds); hnw = A(hg_norm_w)[0]
    ims = []
    for b, hh in cores:
        hs = [hh * 4 + i for i in range(4)]

        def cols(off, g, src):
            return src[:, O3 + off + g * 128:O3 + off + (g + 1) * 128]
        q2 = np.stack([np.stack([np.concatenate([cols(0, g, zc[b]), cols(0, g, zl[b])], 0), np.concatenate([cols(0, g, zl[b]), cols(0, g, zc[b])], 0)]) for g in hs])
        i2 = np.stack([np.stack([np.concatenate([cols(3072, g, zc[b]), cols(3072, g, zl[b])], 0), np.concatenate([cols(3072, g, zl[b]), cols(3072, g, zc[b])], 0)]) for g in hs])
        f2 = np.stack([np.stack([np.concatenate([cols(1024, g, zc[b]), cols(1024, g, zl[b])], 0), np.concatenate([cols(2048, g, zl[b]), cols(2048, g, zc[b])], 0)]) for g in hs])
        ims.append({"qT": np.ascontiguousarray(q2.transpose(0, 1, 3, 2)), "fT": np.ascontiguousarray(f2.transpose(0, 1, 3, 2)),
                    "vtok": np.ascontiguousarray(i2.reshape(4, 2, 18, 128, 128).transpose(0, 1, 3, 2, 4)),
                    "lbraw": np.ascontiguousarray(hgl[:, hh * 512:(hh + 1) * 512].reshape(2, 4, 128).transpose(2, 1, 0)),
                    "nw": np.ascontiguousarray(hnw.reshape(128, 1)),
                    "gT": np.ascontiguousarray(np.stack([zl[b][:, O4 + 1024 + g * 128:O4 + 1024 + (g + 1) * 128].T for g in hs])),
                    "ident": np.eye(128, dtype=np.float32), "mask": np.stack([mF, mB])})
    res = _run(ncH, ims)
    for ci, (b, hh) in enumerate(cores):
        for i in range(4):
            g = hh * 4 + i
            y1[b][:, 1024 + g * 128:1024 + (g + 1) * 128] = res[ci]["yT"][i].T
    del res, ims, zl, zc

    ncF = build_toklocal(0, True, final=True)
    adawF = _blk(ada_w[1][:, 4096:]); adabF = np.ascontiguousarray(ada_b[1][4096:].reshape(16, 128).T)
    woutF = _blk(o_w_out); nwF = _fm(A(final_norm_w))
    zc_rows = np.zeros((256, 2048), np.float32)
    res = _run(ncF, [{"xT": xT1[ci], "csil": csil[ci], "adaw": adawF, "adab": adabF, "normw": nwF,
                      "yT": np.ascontiguousarray(_rows(y1[b], zc_rows, hh).T), "wout": woutF} for ci, (b, hh) in enumerate(cores)])
    out = np.empty((4, 2048, 2048), np.float32)
    for ci, (b, hh) in enumerate(cores):
        out[b, hh * 1024:(hh + 1) * 1024] = res[ci]["oT"][:, :1024].T
    return out
```
